# Optimizing a Trainium2 kernel written in Bass

```python
import math
import jax
import jax.numpy as jnp
from jax import lax
import numpy as np

D_MODEL = 4096
BATCH = 1
SEQ = 8192
DEPTH = 1
DEC_BATCH = 128
DEC_SEQ = 1
PAST_LEN = 2048
PAGE_SIZE = 128

HEAD_DIM = 128
ROT_DIM = HEAD_DIM // 4
ROPE_THETA = 500000.0
DIL_GROUPS = ((128, 1), (512, 4), (2048, 16))
N_GROUPS = len(DIL_GROUPS)
A_HEADS = 4
A_WIDTH = A_HEADS * HEAD_DIM
Q_BLOCK = 128
B_HEADS = 16
B_DK = 128
B_DV = 128
B_QK_WIDTH = B_HEADS * B_DK
B_V_WIDTH = B_HEADS * B_DV
B_CONV = 4
B_CONV_COLS = 2 * B_QK_WIDTH + B_V_WIDTH
DELTA_CHUNK = 64
FFN_CONV = 3
D_FF = 256 * ((8 * D_MODEL // 3 + 255) // 256)
EPS = 1e-6

A_COLS = N_GROUPS * 3 * A_WIDTH
OFF_BQKV = A_COLS
OFF_BZ = OFF_BQKV + B_CONV_COLS
OFF_BBETA = OFF_BZ + B_V_WIDTH
OFF_BALPHA = OFF_BBETA + B_HEADS
OFF_GATE = OFF_BALPHA + B_HEADS
IN_COLS = OFF_GATE + 2 * D_MODEL

kernel_name = "hybrid_dilated_swa_gated_deltanet_convffn_step"

F32 = jnp.float32


def rmsnorm(x, w):
    xf = x.astype(F32)
    y = xf * lax.rsqrt(jnp.mean(xf * xf, axis=-1, keepdims=True) + EPS)
    return (y * w.astype(F32)).astype(x.dtype)


def l2norm(x):
    xf = x.astype(F32)
    return xf * lax.rsqrt(jnp.sum(xf * xf, axis=-1, keepdims=True) + EPS)


def partial_rope(x, pos):
    inv = ROPE_THETA ** (-jnp.arange(0, ROT_DIM, 2, dtype=F32) / ROT_DIM)
    ang = pos.astype(F32)[:, None] * inv[None, :]
    shape = (1, x.shape[1]) + (1,) * (x.ndim - 3) + (ROT_DIM // 2,)
    cos = jnp.cos(ang).reshape(shape)
    sin = jnp.sin(ang).reshape(shape)
    xr = x[..., :ROT_DIM].astype(F32)
    x1, x2 = xr[..., :ROT_DIM // 2], xr[..., ROT_DIM // 2:]
    rot = jnp.concatenate([x1 * cos - x2 * sin, x2 * cos + x1 * sin], axis=-1)
    return jnp.concatenate([rot.astype(x.dtype), x[..., ROT_DIM:]], axis=-1)


def causal_depthwise_conv(buf, x, w):
    width = w.shape[0]
    t = x.shape[1]
    xp = jnp.concatenate([buf.astype(x.dtype), x], axis=1)
    out = xp[:, 0:t] * w[0]
    for i in range(1, width):
        out = out + xp[:, i:i + t] * w[i]
    return out, xp[:, xp.shape[1] - (width - 1):]


def dilated_group_attention(q, k_all, v_all, n_before, window, dil):
    b, t, h, d = q.shape
    n_keys = window // dil + 1
    blk = Q_BLOCK if t % Q_BLOCK == 0 else t
    offs = dil * jnp.arange(n_keys, dtype=jnp.int32)
    qf, kf, vf = q.astype(F32), k_all.astype(F32), v_all.astype(F32)

    def one_block(start):
        qb = lax.dynamic_slice_in_dim(qf, start, blk, axis=1)
        rows = n_before + start + jnp.arange(blk, dtype=jnp.int32)
        idx = rows[:, None] - offs[None, :]
        valid = idx >= 0
        idx = jnp.maximum(idx, 0)
        kb = jnp.take(kf, idx, axis=1)
        vb = jnp.take(vf, idx, axis=1)
        s = jnp.einsum('bqhd,bqkhd->bhqk', qb, kb) * (HEAD_DIM ** -0.5)
        s = jnp.where(valid[None, None], s, -jnp.inf)
        m = jnp.max(s, axis=-1, keepdims=True)
        p = jnp.exp(s - m)
        den = jnp.sum(p, axis=-1, keepdims=True)
        o = jnp.einsum('bhqk,bqkhd->bqhd', p / den, vb)
        lse = (m + jnp.log(den))[..., 0]
        return o, jnp.transpose(lse, (0, 2, 1))

    starts = jnp.arange(t // blk, dtype=jnp.int32) * blk
    o, lse = lax.map(one_block, starts)
    o = jnp.moveaxis(o, 0, 1).reshape(b, t, h, d)
    lse = jnp.moveaxis(lse, 0, 1).reshape(b, t, h)
    return o, lse


def gated_delta_chunked(q, k, v, beta, g, s0):
    b, t, h, dk = q.shape
    dv = v.shape[-1]
    c = min(DELTA_CHUNK, t)
    n = -(-t // c)
    pad = n * c - t

    def to_chunks(a):
        a = a.astype(F32)
        a = jnp.pad(a, [(0, 0), (0, pad)] + [(0, 0)] * (a.ndim - 2))
        a = a.reshape((b, n, c) + a.shape[2:])
        return jnp.moveaxis(a, 3, 1)

    q, k, v, beta, g = (to_chunks(a) for a in (q, k, v, beta, g))
    q = q * (dk ** -0.5)
    gc = jnp.cumsum(g, axis=-1)
    tril = jnp.tril(jnp.ones((c, c), dtype=bool))
    strict = jnp.tril(jnp.ones((c, c), dtype=bool), -1)
    decay = jnp.exp(jnp.where(tril, gc[..., :, None] - gc[..., None, :], -jnp.inf))
    kb = k * beta[..., None]
    a = jnp.where(strict, jnp.einsum('bhnid,bhnjd->bhnij', kb, k) * decay, 0.0)
    eye = jnp.eye(c, dtype=F32)
    tmat = lax.linalg.triangular_solve(eye + a, jnp.broadcast_to(eye, a.shape), left_side=True, lower=True)
    u = jnp.einsum('bhnij,bhnjd->bhnid', tmat, v * beta[..., None])
    w = jnp.einsum('bhnij,bhnjd->bhnid', tmat, kb * jnp.exp(gc)[..., None])
    qk = jnp.einsum('bhnid,bhnjd->bhnij', q, k) * decay
    qg = q * jnp.exp(gc)[..., None]
    kd = k * jnp.exp(gc[..., -1:] - gc)[..., None]
    glast = jnp.exp(gc[..., -1])

    def step(s, xs):
        u_c, w_c, qk_c, qg_c, kd_c, gl_c = xs
        v_new = u_c - jnp.einsum('bhik,bhkv->bhiv', w_c, s)
        o_c = jnp.einsum('bhik,bhkv->bhiv', qg_c, s) + jnp.einsum('bhij,bhjv->bhiv', qk_c, v_new)
        s = s * gl_c[..., None, None] + jnp.einsum('bhik,bhiv->bhkv', kd_c, v_new)
        return s, o_c

    xs = tuple(jnp.moveaxis(arr, 2, 0) for arr in (u, w, qk, qg, kd, glast))
    s_fin, o = lax.scan(step, s0.astype(F32), xs)
    o = jnp.moveaxis(o, 0, 2)
    o = jnp.moveaxis(o, 1, 3).reshape(b, n * c, h, dv)[:, :t]
    return o, s_fin


def trunk_layer(x, pos, kv_bufs, conv_buf, s0, ffn_buf,
                norm_mix, w_in, conv_qkv, a_log, dt_bias, delta_norm,
                w_a_out, w_b_out, w_out, norm_ffn, w_gate, ffn_conv, w_up, w_down):
    b, t, _ = x.shape
    h = rmsnorm(x, norm_mix)
    proj = h @ w_in

    qkv_a = proj[..., :A_COLS].reshape(b, t, N_GROUPS, 3, A_HEADS, HEAD_DIM)
    q_a = partial_rope(qkv_a[:, :, :, 0], pos)
    k_a = partial_rope(qkv_a[:, :, :, 1], pos)
    v_a = qkv_a[:, :, :, 2]
    outs, lses, new_kv = [], [], []
    for gi, (window, dil) in enumerate(DIL_GROUPS):
        buf = kv_bufs[gi].astype(x.dtype)
        k_all = jnp.concatenate([buf[:, :, 0], k_a[:, :, gi]], axis=1)
        v_all = jnp.concatenate([buf[:, :, 1], v_a[:, :, gi]], axis=1)
        o_g, lse_g = dilated_group_attention(q_a[:, :, gi], k_all, v_all, buf.shape[1], window, dil)
        outs.append(o_g)
        lses.append(lse_g)
        keep = min(window, t)
        new_kv.append(jnp.stack([k_a[:, t - keep:, gi], v_a[:, t - keep:, gi]], axis=2))
    mix_w = jax.nn.softmax(jnp.stack(lses), axis=0)
    o_a = jnp.sum(mix_w[..., None] * jnp.stack(outs), axis=0).astype(x.dtype).reshape(b, t, A_WIDTH)

    qkv_b, conv_state = causal_depthwise_conv(conv_buf, proj[..., OFF_BQKV:OFF_BZ], conv_qkv)
    qkv_b = jax.nn.silu(qkv_b)
    q_b = l2norm(qkv_b[..., :B_QK_WIDTH].reshape(b, t, B_HEADS, B_DK))
    k_b = l2norm(qkv_b[..., B_QK_WIDTH:2 * B_QK_WIDTH].reshape(b, t, B_HEADS, B_DK))
    v_b = qkv_b[..., 2 * B_QK_WIDTH:].reshape(b, t, B_HEADS, B_DV)
    z_b = proj[..., OFF_BZ:OFF_BBETA].reshape(b, t, B_HEADS, B_DV)
    beta = jax.nn.sigmoid(proj[..., OFF_BBETA:OFF_BALPHA].astype(F32))
    log_decay = -jnp.exp(a_log.astype(F32)) * jax.nn.softplus(
        proj[..., OFF_BALPHA:OFF_GATE].astype(F32) + dt_bias.astype(F32))
    o_b, s_new = gated_delta_chunked(q_b, k_b, v_b, beta, log_decay, s0)
    o_b = rmsnorm(o_b, delta_norm) * jax.nn.silu(z_b.astype(F32))
    o_b = o_b.astype(x.dtype).reshape(b, t, B_V_WIDTH)

    gates = jax.nn.sigmoid(proj[..., OFF_GATE:])
    merged = gates[..., :D_MODEL] * (o_a @ w_a_out) + gates[..., D_MODEL:] * (o_b @ w_b_out)
    x = x + merged @ w_out

    h2 = rmsnorm(x, norm_ffn)
    g_ffn, ffn_state = causal_depthwise_conv(ffn_buf, h2 @ w_gate, ffn_conv)
    x = x + (jax.nn.silu(g_ffn) * (h2 @ w_up)) @ w_down
    return x, new_kv, conv_state, s_new.astype(s0.dtype), ffn_state


def setup_inputs(seed: int = 0) -> dict:
    key = jax.random.key(seed)
    ks = iter(jax.random.split(key, 40))

    def nrm(shape, scale=1.0):
        return jax.random.normal(next(ks), shape, F32) * scale

    def gain(shape):
        return 1.0 + 0.01 * nrm(shape)

    def buf_len(w):
        return min(w, PAST_LEN)

    a_log = jnp.log(jax.random.uniform(next(ks), (DEPTH, B_HEADS), F32, 1.0, 16.0))
    dt = jnp.exp(jax.random.uniform(next(ks), (DEPTH, B_HEADS), F32, math.log(1e-3), math.log(1e-1)))
    dt_bias = dt + jnp.log(-jnp.expm1(-dt))
    return {
        "x_prompt": nrm((BATCH, SEQ, D_MODEL)),
        "x_sample": nrm((DEC_BATCH, DEC_SEQ, D_MODEL)),
        "cache_kv_w128": nrm((DEPTH, DEC_BATCH, buf_len(DIL_GROUPS[0][0]), 2, A_HEADS, HEAD_DIM)),
        "cache_kv_w512": nrm((DEPTH, DEC_BATCH, buf_len(DIL_GROUPS[1][0]), 2, A_HEADS, HEAD_DIM)),
        "cache_kv_w2048": nrm((DEPTH, DEC_BATCH, buf_len(DIL_GROUPS[2][0]), 2, A_HEADS, HEAD_DIM)),
        "state_conv_qkv": nrm((DEPTH, DEC_BATCH, B_CONV - 1, B_CONV_COLS)),
        "state_delta": nrm((DEPTH, DEC_BATCH, B_HEADS, B_DK, B_DV), 0.5),
        "state_ffn_conv": nrm((DEPTH, DEC_BATCH, FFN_CONV - 1, D_FF)),
        "norm_mix": gain((DEPTH, D_MODEL)),
        "w_in": nrm((DEPTH, D_MODEL, IN_COLS), D_MODEL ** -0.5),
        "conv_qkv": nrm((DEPTH, B_CONV, B_CONV_COLS), B_CONV ** -0.5),
        "a_log": a_log,
        "dt_bias": dt_bias,
        "delta_norm": gain((DEPTH, B_DV)),
        "w_a_out": nrm((DEPTH, A_WIDTH, D_MODEL), A_WIDTH ** -0.5),
        "w_b_out": nrm((DEPTH, B_V_WIDTH, D_MODEL), B_V_WIDTH ** -0.5),
        "w_out": nrm((DEPTH, D_MODEL, D_MODEL), D_MODEL ** -0.5),
        "norm_ffn": gain((DEPTH, D_MODEL)),
        "w_gate": nrm((DEPTH, D_MODEL, D_FF), D_MODEL ** -0.5),
        "ffn_conv": nrm((DEPTH, FFN_CONV, D_FF), FFN_CONV ** -0.5),
        "w_up": nrm((DEPTH, D_MODEL, D_FF), D_MODEL ** -0.5),
        "w_down": nrm((DEPTH, D_FF, D_MODEL), D_FF ** -0.5),
        "norm_final": gain((D_MODEL,)),
    }


def reference(x_prompt, x_sample, cache_kv_w128, cache_kv_w512, cache_kv_w2048, state_conv_qkv, state_delta,
              state_ffn_conv, norm_mix, w_in, conv_qkv, a_log, dt_bias, delta_norm, w_a_out, w_b_out, w_out,
              norm_ffn, w_gate, ffn_conv, w_up, w_down, norm_final):
    bp, tp, _ = x_prompt.shape
    ts = x_sample.shape[1]
    dtype = x_prompt.dtype
    pos_p = jnp.arange(tp, dtype=jnp.int32)
    pos_s = PAST_LEN + jnp.arange(ts, dtype=jnp.int32)
    empty_kv = jnp.zeros((bp, 0, 2, A_HEADS, HEAD_DIM), dtype)
    conv0 = jnp.zeros((bp, B_CONV - 1, B_CONV_COLS), dtype)
    s0 = jnp.zeros((bp, B_HEADS, B_DK, B_DV), dtype)
    ffn0 = jnp.zeros((bp, FFN_CONV - 1, D_FF), dtype)
    hp, hs = x_prompt, x_sample
    p_new = [[] for _ in range(6)]
    s_new = [[] for _ in range(6)]
    for layer in range(DEPTH):
        weights = (norm_mix[layer], w_in[layer], conv_qkv[layer], a_log[layer], dt_bias[layer], delta_norm[layer],
                   w_a_out[layer], w_b_out[layer], w_out[layer], norm_ffn[layer], w_gate[layer], ffn_conv[layer],
                   w_up[layer], w_down[layer])
        hp, kv_p, conv_p, delta_p, ffn_p = trunk_layer(
            hp, pos_p, (empty_kv, empty_kv, empty_kv), conv0, s0, ffn0, *weights)
        hs, kv_s, conv_s, delta_s, ffn_s = trunk_layer(
            hs, pos_s, (cache_kv_w128[layer], cache_kv_w512[layer], cache_kv_w2048[layer]),
            state_conv_qkv[layer], state_delta[layer], state_ffn_conv[layer], *weights)
        for lst, val in zip(p_new, (kv_p[0], kv_p[1], kv_p[2], conv_p, delta_p, ffn_p)):
            lst.append(val)
        for lst, val in zip(s_new, (kv_s[0], kv_s[1], kv_s[2], conv_s, delta_s, ffn_s)):
            lst.append(val)
    y_prompt = rmsnorm(hp, norm_final)
    y_sample = rmsnorm(hs, norm_final)
    kv128_p, kv512_p, kv2048_p, conv_qkv_p, delta_p, ffn_conv_p = [jnp.stack(l) for l in p_new]
    kv128_s, kv512_s, kv2048_s, conv_qkv_s, delta_s, ffn_conv_s = [jnp.stack(l) for l in s_new]
    return (y_prompt, y_sample, kv128_p, kv512_p, kv2048_p, conv_qkv_p, delta_p, ffn_conv_p,
            kv128_s, kv512_s, kv2048_s, conv_qkv_s, delta_s, ffn_conv_s)
```

```python
import contextlib
import numpy as np
import concourse.bass as bass
import concourse.mybir as mybir
from concourse.bass_utils import run_bass_kernel_spmd

F32 = mybir.dt.float32
BF16 = mybir.dt.bfloat16
AF = mybir.ActivationFunctionType
ALU = mybir.AluOpType
AX = mybir.AxisListType

NCORES = 8
D = 4096
KC = 32
SEQ = 8192
OWN = SEQ // NCORES
NS = 16
NPRE = 128
NT = NPRE + OWN + NS
NP = NPRE + OWN
HIST = 2048
NTILE_A = (HIST + NP) // 128
PAST = 2048
HD = 128
ROT = 32
A_HEADS = 4
NG = 3
DILS = ((128, 1), (512, 4), (2048, 16))
RMAX = (1, 4, 16)
A_COLS = NG * 3 * A_HEADS * HD
B_HEADS = 16
OFF_BQKV = A_COLS
OFF_BZ = OFF_BQKV + 3 * 2048
OFF_BBETA = OFF_BZ + 2048
OFF_BALPHA = OFF_BBETA + 16
OFF_GATE = OFF_BALPHA + 16
IN_COLS = OFF_GATE + 2 * D
DFF = 11008
EPS = 1e-6
TOK_TILES = ((0, 512), (512, 512), (1024, NT - 1024))
NORM_TILES = ((0, 256), (256, 256), (512, 256), (768, 256), (1024, NT - 1024))
RUN_A = True
RUN_B = True
RUN_F = True
P1_CHUNKS = 111
DBG_SKIP_CHUNK = False
DBG_SKIP_P2 = False
DBG_SKIP_SAMPLE = False
DBG_SAMPLE_NOCHUNK = False
DBG_NSAMP = NS
NALL_DECL = 7168


class Prog:
    EPOCH = 12000
    BLK = {"pe": "tensor", "act": "scalar", "dve": "vector", "pool": "gpsimd", "sp": "sync"}

    def __init__(self, nc, stack):
        self.nc = nc
        self.stack = stack
        self.streams = {k: [] for k in self.BLK}
        self.count = {k: 0 for k in self.BLK}
        self.esems = {k: [] for k in self.BLK}
        self.seen = {k: {} for k in self.BLK}
        self.lastw = {}
        self.readers = {}
        self.dsems = []
        self.dcount = []
        self.dnext = 0
        self.NDS = 24
        self.pending = []
        for E, n in (("pe", 16), ("act", 6), ("dve", 6), ("pool", 4)):
            for i in range(n):
                self.esems[E].append(self._sem(f"s_{E}_{i}"))
        for i in range(self.NDS):
            self.dsems.append(self._sem(f"s_dma_{i}"))
            self.dcount.append(0)

    def _sem(self, name):
        return self.stack.enter_context(self.nc.semaphore(name))

    def _newtok(self, E):
        self.count[E] += 1
        n = self.count[E]
        ep = (n - 1) // self.EPOCH
        while len(self.esems[E]) <= ep:
            self.esems[E].append(self._sem(f"s_{E}_{len(self.esems[E])}"))
        return (self.esems[E][ep], n - ep * self.EPOCH)

    def _collect(self, E, reads, writes):
        toks = []
        for r in reads:
            t = self.lastw.get(r)
            if t is not None:
                toks.append(t)
        for w in writes:
            t = self.lastw.get(w)
            if t is not None:
                toks.append(t)
            for t in self.readers.get(w, {}).values():
                toks.append(t)
        own = set(id(s) for s in self.esems[E])
        out = []
        seen = self.seen[E]
        for (s, v) in toks:
            if id(s) in own and E == "pe":
                continue
            if seen.get(id(s), 0) >= v:
                continue
            seen[id(s)] = v
            out.append((s, v))
        return out

    def _update(self, tok, reads, writes):
        for r in reads:
            d = self.readers.setdefault(r, {})
            d[id(tok[0])] = tok
        for w in writes:
            self.lastw[w] = tok
            self.readers[w] = {}

    def op(self, E, fn, reads=(), writes=()):
        waits = self._collect(E, reads, writes)
        tok = self._newtok(E)
        self.streams[E].append((waits, fn, tok, 1))
        self._update(tok, reads, writes)
        return tok

    def dma(self, Q, out, in_, reads=(), writes=()):
        waits = self._collect(Q, reads, writes)
        i = self.dnext % self.NDS
        self.dnext += 1
        if len(self.dsems) <= i:
            self.dsems.append(self._sem(f"s_dma_{i}"))
            self.dcount.append(0)
        s = self.dsems[i]
        if self.dcount[i] > 0 and self.seen[Q].get(id(s), 0) < 16 * self.dcount[i]:
            waits.append((s, 16 * self.dcount[i]))
            self.seen[Q][id(s)] = 16 * self.dcount[i]
        self.dcount[i] += 1
        tok = (s, 16 * self.dcount[i])
        self.streams[Q].append((waits, lambda e: e.dma_start(out=out, in_=in_), tok, 16))
        self._update(tok, reads, writes)
        return tok

    def store(self, out, in_, reads=(), writes=()):
        self.pending.append((out, in_, tuple(reads), tuple(writes)))

    def flush_stores(self):
        pend, self.pending = self.pending, []
        for (out, in_, reads, writes) in pend:
            self.dma("sp", out, in_, reads=reads, writes=writes)

    def flush(self):
        self.flush_stores()
        for i, s in enumerate(self.dsems):
            v = 16 * self.dcount[i]
            if v and self.seen["sp"].get(id(s), 0) < v:
                self.streams["sp"].append(([(s, v)], None, None, 0))
                self.seen["sp"][id(s)] = v
        for E in ("pe", "act", "dve", "pool"):
            if self.count[E]:
                n = self.count[E]
                ep = (n - 1) // self.EPOCH
                s = self.esems[E][ep]
                self.streams["sp"].append(([(s, n - ep * self.EPOCH)], None, None, 0))
        streams = self.streams
        self.streams = {k: [] for k in self.BLK}
        with self.nc.Block() as block:
            for E, bname in self.BLK.items():
                lst = streams[E]
                if not lst:
                    continue

                def body(e, lst=lst):
                    for waits, fn, tok, inc in lst:
                        for (s, v) in waits:
                            e.wait_ge(s, v)
                        if fn is not None:
                            fn(e).then_inc(tok[0], inc)

                getattr(block, bname)(body)
        self.lastw = {}
        self.readers = {}


def _rope_from_pos(pos):
    inv = (500000.0 ** (-np.arange(0, ROT, 2, dtype=np.float32) / ROT)).astype(np.float32)
    ang = pos.astype(np.float32)[None, :] * inv[:, None]
    cos = np.cos(ang).astype(np.float32)
    sin = np.sin(ang).astype(np.float32)
    return (np.ascontiguousarray(np.concatenate([cos, cos], 0)),
            np.ascontiguousarray(np.concatenate([-sin, sin], 0)))


def _rope_tables(core):
    pos = np.concatenate([np.arange(core * OWN - NPRE, core * OWN + OWN), np.full((NS,), PAST)])
    return _rope_from_pos(pos)


def _rope_tables_hist(core):
    pos = np.arange(core * OWN - NPRE - HIST, core * OWN - NPRE)
    return _rope_from_pos(pos)


def _perm32():
    p = np.zeros((128, 32), np.float32)
    for m in range(32):
        p[(m + 16) % 32, m] = 1.0
    return p


def _attn_masks():
    n = sum(r + 1 for r in RMAX)
    m = np.zeros((128, n, 128), np.float32)
    k = np.arange(128)[:, None]
    q = np.arange(128)[None, :]
    idx = 0
    for g, (w, dil) in enumerate(DILS):
        for r in range(RMAX[g] + 1):
            delta = r * 128 + q - k
            m[:, idx, :] = ((delta >= 0) & (delta <= w) & (delta % dil == 0)).astype(np.float32)
            idx += 1
    return m


def _tile_w(w, c0, c1):
    K = w.shape[0]
    kc = K // 128
    sub = w[:, c0:c1]
    ncol = c1 - c0
    ncb = (ncol + 127) // 128
    if ncb * 128 != ncol:
        sub = np.concatenate([sub, np.zeros((K, ncb * 128 - ncol), np.float32)], 1)
    return np.ascontiguousarray(sub.reshape(kc, 128, ncb, 128).transpose(2, 1, 0, 3))


def _fm(x2d):
    t, f = x2d.shape
    return np.ascontiguousarray(x2d.reshape(t, f // 128, 128).transpose(2, 1, 0))


def build_program():
    nc = bass.Bass("TRN2", target_bir_lowering=False)

    declared = []

    def inp(name, shape, dt=F32):
        declared.append(name)
        return nc.dram_tensor(name, list(shape), dt, kind="ExternalInput").ap()

    def outp(name, shape, dt=F32):
        return nc.dram_tensor(name, list(shape), dt, kind="ExternalOutput").ap()

    def scratch(name, shape, dt=F32):
        return nc.dram_tensor(name, list(shape), dt, kind="Internal").ap()

    xT = inp("xT", (128, KC, NT))
    normw = inp("normw", (128, 3 * KC))
    ident = inp("ident", (128, 128))
    if RUN_A:
        xhT = inp("xhT", (128, KC, HIST))
        w_a = inp("w_a", (A_COLS // 128, 128, KC, 128))
        cosT = inp("cosT", (32, NT))
        sinT = inp("sinT", (32, NT))
        coshT = inp("coshT", (32, HIST))
        sinhT = inp("sinhT", (32, HIST))
        perm = inp("perm", (128, 32))
        maskT = inp("maskT", (128, 24, 128))
        validf = inp("validf", (128, NTILE_A))
        ck = [inp(f"cache{g}", (NS, DILS[g][0], 2, A_HEADS, HD)) for g in range(NG)]
        o_qkv = outp("o_qkv", (NG * 3 * A_HEADS, 128, NT))
        o_oa = outp("o_oa", (128, A_HEADS, NT))
        kT_s = scratch("kT_s", (NG * A_HEADS, 128, NTILE_A * 128), BF16)
        vtok_s = scratch("vtok_s", (NG * A_HEADS, NTILE_A, 128, 128), BF16)
        qT_s = scratch("qT_s", (NG * A_HEADS, 128, NP), BF16)
    if RUN_A and RUN_B and RUN_F:
        w_g = inp("w_g", (64, 128, KC, 128))
        w_ao = inp("w_ao", (32, 128, 4, 128))
        w_bo = inp("w_bo", (32, 128, 16, 128))
        w_o = inp("w_o", (32, 128, KC, 128))
        w_gu = inp("w_gu", (172, 128, KC, 128))
        w_d = inp("w_d", (32, 128, 86, 128))
        fcw = inp("fcw", (128, 86, 3))
        fsth = inp("fsth", (128, 86, 2, NS))
        fst_in = inp("fst_in", (NS, 2, DFF))
        gate_s = scratch("gate_s", (64, 128, NT), BF16)
        xmid_s = scratch("xmid_s", (KC, 128, NT))
        rs_s = scratch("rs_s", (128, NT))
        y_s = scratch("y_s", (KC, 128, NT - (NPRE - 2)))
        o_y = outp("o_y", (KC, 128, NT - (NPRE - 2)))
        o_g = outp("o_g", (86, 128, 2 + NS))
        o_fold = outp("o_fold", (NS, DFF))
    if RUN_B:
        NALL = NALL_DECL
        xallT = inp("xallT", (128, KC, NALL))
        w_b = inp("w_b", (65, 128, KC, 128))
        convw = inp("convw", (128, 48, 4))
        bg48 = inp("bg48", (48, 2))
        dnorm = inp("dnorm", (128, 1))
        eoh = inp("eoh", (48, 16))
        sel63 = inp("sel63", (64, 128))
        maskS = inp("maskS", (64, 2, 64))
        rmask = inp("rmask", (48, 2, 512))
        cmask = inp("cmask", (128, 8))
        csth = inp("csth", (128, 48, 3, NS))
        cst_in = inp("cst_in", (NS, 3, 6144))
        o_cst_old = outp("o_cst_old", (NS, 2, 6144))
        sdel = inp("sdel", (NS, 128, B_HEADS, 128))
        o_S = outp("o_S", (128, B_HEADS, 128))
        o_Ss = outp("o_Ss", (NS, 128, B_HEADS, 128))
        o_bpre = outp("o_bpre", (48, 128, 3 + NS))
        o_ob = outp("o_ob", (B_HEADS, 128, NT))
        ob_s = scratch("ob_s", (B_HEADS, 128, NT), BF16)

    with contextlib.ExitStack() as stack:
        P = Prog(nc, stack)

        uid = {"n": 0}

        def sb(name, shape, dt=F32, ctx=None):
            uid["n"] += 1
            return (ctx or stack).enter_context(nc.sbuf_tensor(f"{name}_u{uid['n']}", list(shape), dt))

        ones_bf = sb("ones_bf", (128, 128), BF16)
        ones_f = sb("ones_f", (128, 128))
        normw_sb = sb("normw_sb", (128, 3 * KC))
        perm_sb = sb("perm_sb", (128, 32))
        pbig = stack.enter_context(nc.psum_tensor("pbig", [128, 4096], F32))
        psb = [pbig[:, i * 512:(i + 1) * 512] for i in range(8)]

        P.op("pool", lambda e: e.memset(ones_bf[:], 1.0), writes=["ones"])
        P.op("pool", lambda e: e.memset(ones_f[:], 1.0), writes=["onesf"])
        P.dma("sp", normw_sb[:], normw, writes=["normw"])
        if RUN_A:
            P.dma("sp", perm_sb[:], perm, writes=["perm"])
        ident_sb = sb("ident_sb", (128, 128))
        P.dma("sp", ident_sb[:], ident, writes=["ident"])

        def rmsnorm_to(dst_fn, dst_res, src_dram, nw_off, tiles, xbuf, sqbuf, rstd):
            for (t0, n) in tiles:
                for q4 in range(4):
                    P.dma("sp", xbuf[:, q4 * 8:(q4 + 1) * 8, :n], src_dram[:, q4 * 8:(q4 + 1) * 8, t0:t0 + n],
                          writes=[f"xb{q4}"])
                for q4 in range(4):
                    P.op("act", lambda e, q4=q4, n=n: e.activation(
                        out=sqbuf[:, q4 * 8:(q4 + 1) * 8, :n], in_=xbuf[:, q4 * 8:(q4 + 1) * 8, :n], func=AF.Square),
                        reads=[f"xb{q4}"], writes=[f"sq{q4}"])
                for kc in range(KC):
                    P.op("pe", lambda e, kc=kc, n=n: e.matmul(psb[7][:, :n], lhsT=ones_bf[:], rhs=sqbuf[:, kc, :n],
                                                             start=(kc == 0), stop=(kc == KC - 1)),
                         reads=["ones", f"sq{kc // 8}"], writes=["ps7"])
                P.op("dve", lambda e, n=n: e.tensor_scalar(out=rstd[:, :n], in0=psb[7][:, :n], scalar1=1.0 / D,
                                                          scalar2=EPS, op0=ALU.mult, op1=ALU.add),
                     reads=["ps7"], writes=["rstd"])
                P.op("act", lambda e, n=n: e.activation(out=rstd[:, :n], in_=rstd[:, :n], func=AF.Sqrt),
                     reads=["rstd"], writes=["rstd"])
                P.op("dve", lambda e, n=n: e.reciprocal(out=rstd[:, :n], in_=rstd[:, :n]),
                     reads=["rstd"], writes=["rstd"])
                for kc in range(KC):
                    P.op("dve", lambda e, kc=kc, n=n, t0=t0: e.scalar_tensor_tensor(
                        out=dst_fn(kc, t0, n), in0=xbuf[:, kc, :n],
                        scalar=normw_sb[:, nw_off + kc:nw_off + kc + 1], in1=rstd[:, :n],
                        op0=ALU.mult, op1=ALU.mult),
                        reads=[f"xb{kc // 8}", "rstd", "normw"], writes=[dst_res(t0)])

        wst = [sb(f"wst{i}", (128, 8, 128)) for i in range(2)]
        NWS = 3
        wbf = [sb(f"wbf{i}", (128, KC, 128), BF16) for i in range(NWS)]
        lin = {"slot": 0, "st": 0, "bank": 0, "ce": 0}

        def load_w(piece, kn):
            slot = lin["slot"] % NWS
            lin["slot"] += 1
            for k0 in range(0, kn, 8):
                kk = min(8, kn - k0)
                st = lin["st"] % 2
                lin["st"] += 1
                P.dma("sp", wst[st][:, :kk, :], piece[:, k0:k0 + kk, :], writes=[f"wst{st}"])
                ce = ("act", "pool")[lin["ce"] % 2]
                lin["ce"] += 1
                if ce == "act":
                    P.op("act", lambda e, st=st, slot=slot, k0=k0, kk=kk: e.copy(
                        out=wbf[slot][:, k0:k0 + kk, :], in_=wst[st][:, :kk, :]),
                        reads=[f"wst{st}"], writes=[f"wbf{slot}"])
                else:
                    P.op("pool", lambda e, st=st, slot=slot, k0=k0, kk=kk: e.tensor_copy(
                        out=wbf[slot][:, k0:k0 + kk, :], in_=wst[st][:, :kk, :]),
                        reads=[f"wst{st}"], writes=[f"wbf{slot}"])
            return slot

        def linear(w_dram, cbs, KCn, rhs_fn, rhs_res, tiles, epilogue):
            for cb in cbs:
                pieces = [(k0, min(KC, KCn - k0)) for k0 in range(0, KCn, KC)]
                slots = [load_w(w_dram[cb, :, k0:k0 + kn, :], kn) for (k0, kn) in pieces]
                P.flush_stores()
                for ti, (t0, n) in enumerate(tiles):
                    b = lin["bank"] % 4
                    lin["bank"] += 1
                    for pi, (k0, kn) in enumerate(pieces):
                        for kc in range(kn):
                            P.op("pe", lambda e, b=b, sl=slots[pi], kc=kc, k0=k0, t0=t0, n=n: e.matmul(
                                psb[b][:, :n], lhsT=wbf[sl][:, kc, :], rhs=rhs_fn(k0 + kc, t0, n),
                                start=(k0 + kc == 0), stop=(k0 + kc == KCn - 1)),
                                reads=[f"wbf{slots[pi]}", rhs_res(t0)], writes=[f"ps{b}"])
                    epilogue(cb, ti, t0, n, psb[b][:, :n], f"ps{b}")

        def linear_tm(w_dram, cbs, act, act_res, tok_tiles128, epilogue):
            for cb in cbs:
                slot = load_w(w_dram[cb], KC)
                P.flush_stores()
                for (t0, tidx) in tok_tiles128:
                    b = lin["bank"] % 4
                    lin["bank"] += 1
                    for kc in range(KC):
                        P.op("pe", lambda e, b=b, slot=slot, kc=kc, t0=t0: e.matmul(
                            psb[b][:, :128], lhsT=act[:, kc, t0:t0 + 128], rhs=wbf[slot][:, kc, :],
                            start=(kc == 0), stop=(kc == KC - 1)),
                            reads=[f"wbf{slot}", act_res(t0)], writes=[f"ps{b}"])
                    epilogue(cb, tidx, psb[b][:, :128], f"ps{b}")
                    P.flush_stores()

        def rope(dst, dres, t0, n, cs, sn, c0, rtmp, r):
            P.op("pe", lambda e: e.matmul(psb[4 + r][:32, :n], lhsT=perm_sb[:], rhs=dst[:, t0:t0 + n],
                                          start=True, stop=True),
                 reads=["perm", dres], writes=[f"ps{4 + r}"])
            P.op("dve", lambda e: e.tensor_tensor(out=rtmp[r][:, :n], in0=psb[4 + r][:32, :n],
                                                  in1=sn[:, c0 + t0:c0 + t0 + n], op=ALU.mult),
                 reads=[f"ps{4 + r}", "sin"], writes=[f"rtmp{r}"])
            P.op("dve", lambda e: e.tensor_tensor(out=dst[:32, t0:t0 + n], in0=dst[:32, t0:t0 + n],
                                                  in1=cs[:, c0 + t0:c0 + t0 + n], op=ALU.mult),
                 reads=[dres, "cos"], writes=[dres])
            P.op("dve", lambda e: e.tensor_tensor(out=dst[:32, t0:t0 + n], in0=dst[:32, t0:t0 + n],
                                                  in1=rtmp[r][:, :n], op=ALU.add),
                 reads=[dres, f"rtmp{r}"], writes=[dres])

        est = {"i": 0}

        bstack = contextlib.ExitStack()
        S_own = sb("S_own", (128, B_HEADS, 128), F32, bstack)
        P.op("pool", lambda e: e.memset(S_own[:], 0.0), writes=["S_own"])
        halo = sb("halo", (128, 48, 3), F32, bstack)
        S = sb("S", (128, B_HEADS, 128), F32, bstack)
        Sb = sb("Sb", (128, B_HEADS, 128), BF16, bstack)
        HB = 8

        def bsetup(ph, NTK, wo):
            xbuf = sb("xbuf", (128, KC, 64), F32, ph)
            sqbuf = sb("sqbuf", (128, KC, 64), BF16, ph)
            rstd = sb("rstd", (128, 64), F32, ph)
            hTt = sb("hTt", (128, KC, NTK), BF16, ph)
            convw_sb = sb("convw_sb", (128, 48, 4), F32, ph)
            bg_sb = sb("bg_sb", (48, 2), F32, ph)
            negA = sb("negA", (48, 1), F32, ph)
            dn_sb = sb("dn_sb", (128, 1), F32, ph)
            eoh_sb = sb("eoh_sb", (48, 16), F32, ph)
            sel_sb = sb("sel_sb", (64, 128), F32, ph)
            mS_sb = sb("mS_sb", (64, 2, 64), F32, ph)
            rm_sb = sb("rm_sb", (48, 2, 512), F32, ph)
            cm_sb = sb("cm_sb", (128, 8), F32, ph)
            identb = sb("identb", (128, 128), BF16, ph)
            R = sb("R", (48, NTK), F32, ph)
            tmpg = sb("tmpg", (48, NTK), F32, ph)
            pre = [sb(f"pre{i}", (128, 3 + NTK), F32, ph) for i in range(2)]
            acc = [sb(f"acc{i}", (128, NTK), F32, ph) for i in range(2)]
            sil = [sb(f"sil{i}", (128, NTK), F32, ph) for i in range(2)]
            sqb = sb("sqb", (128, NTK), BF16, ph)
            lnb = sb("lnb", (128, NTK), F32, ph)
            kT = sb("kT", (128, B_HEADS, NTK), BF16, ph)
            vT = sb("vT", (128, B_HEADS, NTK), BF16, ph)
            RT = sb("RT", (64, 48), F32, ph)
            glb = sb("glb", (128, 48), F32, ph)
            egc = sb("egc", (64, 16), F32, ph)
            bege = sb("bege", (64, 16), F32, ph)
            ekd = sb("ekd", (64, 16), F32, ph)
            glx = sb("glx", (128, 16), F32, ph)
            Gm = sb("Gm", (64, HB, 64), F32, ph)
            GT = sb("GT", (64, HB, 64), F32, ph) if wo else None
            dmb = sb("dmb", (64, HB, 64), F32, ph)
            Abuf = [sb(f"Abuf{i}", (64, HB, 64), BF16, ph) for i in range(4)]
            Tt = sb("Tt", (64, HB, 64), F32, ph)
            Ttb = sb("Ttb", (64, HB, 64), BF16, ph)
            Kb = sb("Kb", (64, HB, 128), BF16, ph)
            kd = sb("kd", (64, HB, 128), BF16, ph)
            Vb = sb("Vb", (64, HB, 128), BF16, ph)
            u = sb("u", (64, HB, 128), F32, ph)
            WT = sb("WT", (128, HB, 64), BF16, ph)
            vn = sb("vn", (64, HB, 128), BF16, ph)
            Mq = sb("Mq", (64, HB, 64), BF16, ph) if wo else None
            egb = sb("egb", (128, HB, 64), F32, ph) if wo else None
            qg = sb("qg", (128, HB, 64), BF16, ph) if wo else None

            P.dma("sp", convw_sb[:], convw, writes=["convw"])
            P.dma("sp", bg_sb[:], bg48, writes=["bg"])
            P.dma("sp", dn_sb[:], dnorm, writes=["dn"])
            P.dma("sp", eoh_sb[:], eoh, writes=["eoh"])
            P.dma("sp", sel_sb[:], sel63, writes=["sel"])
            P.dma("sp", mS_sb[:], maskS, writes=["mS"])
            P.dma("sp", rm_sb[:], rmask, writes=["rm"])
            P.dma("sp", cm_sb[:], cmask, writes=["cm"])
            P.op("act", lambda e: e.copy(out=identb[:], in_=ident_sb[:]), reads=["ident"], writes=["identb"])
            P.op("pool", lambda e: e.memset(halo[:], 0.0), writes=["halo"])
            P.op("pool", lambda e: e.memset(S[:], 0.0), writes=["S0", "S1"])
            P.op("pool", lambda e: e.memset(Sb[:], 0.0), writes=["Sb0", "Sb1"])
            P.op("pool", lambda e: e.memset(R[:], 0.0), writes=["R"])
            P.op("pool", lambda e: e.memset(tmpg[:], 0.0), writes=["tmpg"])
            P.op("act", lambda e: e.activation(out=negA[32:48, :], in_=bg_sb[32:48, 1:2], func=AF.Exp),
                 reads=["bg"], writes=["negA"])
            P.op("dve", lambda e: e.tensor_scalar(out=negA[32:48, :], in0=negA[32:48, :], scalar1=-1.0, scalar2=None,
                                                  op0=ALU.mult), reads=["negA"], writes=["negA"])

            pbf = [psb[i].bitcast(BF16) for i in range(8)]
            pbU = pbig[:, 5 * 512:7 * 512]
            bctr = {"i": 0}

            def bproj_epilogue(n, npr, qT_t, last_tile, xs):
                def ep(cb, ti, t0, n_, psap, pres):
                    if cb == 48:
                        P.op("act", lambda e: e.activation(out=R[0:16, :n], in_=psap[0:16, :], func=AF.Sigmoid),
                             reads=[pres], writes=["R"])
                        P.op("act", lambda e: e.activation(out=tmpg[32:48, :n], in_=psap[32:48, :], func=AF.Exp,
                                                           bias=bg_sb[32:48, 0:1], scale=1.0),
                             reads=[pres, "bg"], writes=["tmpg"])
                        P.op("act", lambda e: e.activation(out=tmpg[32:48, :n], in_=tmpg[32:48, :n], func=AF.Ln,
                                                           bias=1.0, scale=1.0), reads=["tmpg"], writes=["tmpg"])
                        P.op("dve", lambda e: e.tensor_scalar(out=tmpg[32:48, :n], in0=tmpg[32:48, :n],
                                                              scalar1=negA[32:48, 0:1], scalar2=None, op0=ALU.mult),
                             reads=["tmpg", "negA"], writes=["tmpg"])
                        P.op("dve", lambda e: e.tensor_tensor_scan(
                            out=R[32:48, :n], data0=rm_sb[32:48, 1 if last_tile else 0, :n], data1=tmpg[32:48, :n],
                            initial=0.0, op0=ALU.mult, op1=ALU.add), reads=["tmpg", "rm", "R"], writes=["R"])
                        return
                    which, h = cb // 16, cb % 16
                    i = bctr["i"] % 2
                    bctr["i"] += 1
                    pr, ac, sl = pre[i], acc[i], sil[i]
                    P.op("act", lambda e: e.copy(out=pr[:, 3:3 + n], in_=psap), reads=[pres], writes=[f"pre{i}"])
                    P.op("pool", lambda e: e.tensor_copy(out=pr[:, 0:3], in_=halo[:, cb, :]), reads=["halo"],
                         writes=[f"pre{i}"])
                    P.op("pool", lambda e: e.tensor_copy(out=halo[:, cb, :], in_=pr[:, npr:npr + 3]),
                         reads=[f"pre{i}"], writes=["halo"])
                    if last_tile:
                        P.store(o_bpre[cb], pr[:, npr:npr + 3 + NS], reads=[f"pre{i}"])
                    P.op("dve", lambda e: e.tensor_scalar(out=ac[:, :n], in0=pr[:, 0:n], scalar1=convw_sb[:, cb, 0:1],
                                                          scalar2=None, op0=ALU.mult),
                         reads=[f"pre{i}", "convw"], writes=[f"acc{i}"])
                    for r in (1, 2, 3):
                        P.op("dve", lambda e, r=r: e.scalar_tensor_tensor(
                            out=ac[:, :n], in0=pr[:, r:r + n], scalar=convw_sb[:, cb, r:r + 1], in1=ac[:, :n],
                            op0=ALU.mult, op1=ALU.add), reads=[f"pre{i}", "convw", f"acc{i}"], writes=[f"acc{i}"])
                    if last_tile:
                        cs = xs["csth"]
                        P.op("dve", lambda e: e.tensor_scalar(out=ac[:, npr:npr + NS], in0=cs[:, cb, 0, :],
                                                              scalar1=convw_sb[:, cb, 0:1], scalar2=None, op0=ALU.mult),
                             reads=["csth", "convw", f"acc{i}"], writes=[f"acc{i}"])
                        for r in (1, 2):
                            P.op("dve", lambda e, r=r: e.scalar_tensor_tensor(
                                out=ac[:, npr:npr + NS], in0=cs[:, cb, r, :], scalar=convw_sb[:, cb, r:r + 1],
                                in1=ac[:, npr:npr + NS], op0=ALU.mult, op1=ALU.add),
                                reads=["csth", "convw", f"acc{i}"], writes=[f"acc{i}"])
                        P.op("dve", lambda e: e.scalar_tensor_tensor(
                            out=ac[:, npr:npr + NS], in0=pr[:, 3 + npr:3 + npr + NS], scalar=convw_sb[:, cb, 3:4],
                            in1=ac[:, npr:npr + NS], op0=ALU.mult, op1=ALU.add),
                            reads=[f"pre{i}", "convw", f"acc{i}"], writes=[f"acc{i}"])
                    if which == 2:
                        P.op("act", lambda e: e.activation(out=vT[:, h, :n], in_=ac[:, :n], func=AF.Silu),
                             reads=[f"acc{i}"], writes=["vT"])
                        if last_tile:
                            P.op("act", lambda e: e.activation(out=xs["vs_f"][:, h, :], in_=ac[:, npr:npr + NS],
                                                               func=AF.Silu), reads=[f"acc{i}"], writes=["vs_f"])
                        return
                    P.op("act", lambda e: e.activation(out=sl[:, :n], in_=ac[:, :n], func=AF.Silu),
                         reads=[f"acc{i}"], writes=[f"sil{i}"])
                    P.op("act", lambda e: e.activation(out=sqb[:, :n], in_=sl[:, :n], func=AF.Square),
                         reads=[f"sil{i}"], writes=["sqb"])
                    P.op("pe", lambda e: e.matmul(psb[7][:, :n], lhsT=ones_bf[:], rhs=sqb[:, :n], start=True, stop=True),
                         reads=["ones", "sqb"], writes=["ps7"])
                    P.op("act", lambda e: e.activation(out=lnb[:, :n], in_=psb[7][:, :n], func=AF.Ln, bias=EPS,
                                                       scale=1.0), reads=["ps7"], writes=["lnb"])
                    qb = float(np.log(128.0 ** -0.5)) if which == 0 else 0.0
                    P.op("act", lambda e: e.activation(out=lnb[:, :n], in_=lnb[:, :n], func=AF.Exp, bias=qb,
                                                       scale=-0.5), reads=["lnb"], writes=["lnb"])
                    dstT = qT_t if which == 0 else kT
                    P.op("dve", lambda e: e.tensor_tensor(out=dstT[:, h, :n], in0=sl[:, :n], in1=lnb[:, :n],
                                                          op=ALU.mult),
                         reads=[f"sil{i}", "lnb"], writes=["qT" if which == 0 else "kT"])
                    if last_tile:
                        dsf = xs["qs_f"] if which == 0 else xs["ks_f"]
                        P.op("dve", lambda e: e.tensor_tensor(out=dsf[:, h, :], in0=sl[:, npr:npr + NS],
                                                              in1=lnb[:, npr:npr + NS], op=ALU.mult),
                             reads=[f"sil{i}", "lnb"], writes=["qs_f" if which == 0 else "ks_f"])
                return ep

            def chunk(c0, with_o, qT_t, oT_t):
                P.op("pe", lambda e: e.transpose(psb[7][0:64, 0:48], R[0:48, c0:c0 + 64], ident_sb[0:48, 0:48]),
                     reads=["R", "ident"], writes=["ps7"])
                P.op("act", lambda e: e.copy(out=RT[:], in_=psb[7][0:64, 0:48]), reads=["ps7"], writes=["RT"])
                P.op("pe", lambda e: e.matmul(psb[7][:, 64:112], lhsT=sel_sb[:], rhs=RT[:], start=True, stop=True),
                     reads=["sel", "RT"], writes=["ps7"])
                P.op("act", lambda e: e.copy(out=glb[:], in_=psb[7][:, 64:112]), reads=["ps7"], writes=["glb"])
                P.op("act", lambda e: e.activation(out=egc[:], in_=RT[:, 32:48], func=AF.Exp), reads=["RT"],
                     writes=["egc"])
                P.op("dve", lambda e: e.tensor_tensor(out=bege[:], in0=RT[:, 0:16], in1=egc[:], op=ALU.mult),
                     reads=["RT", "egc"], writes=["bege"])
                P.op("dve", lambda e: e.tensor_tensor(out=ekd[:], in0=glb[0:64, 32:48], in1=RT[:, 32:48],
                                                      op=ALU.subtract), reads=["glb", "RT"], writes=["ekd"])
                P.op("act", lambda e: e.activation(out=ekd[:], in_=ekd[:], func=AF.Exp), reads=["ekd"], writes=["ekd"])
                P.op("act", lambda e: e.activation(out=glx[:], in_=glb[:, 32:48], func=AF.Exp), reads=["glb"],
                     writes=["glx"])
                for hb in range(B_HEADS // HB):
                    chunk_batch(hb, c0, with_o, qT_t, oT_t)

            def chunk_batch(hb, c0, with_o, qT_t, oT_t):
                if True:
                    h0 = hb * HB
                    v3 = lambda ap, w: ap.rearrange("p (h x) -> p h x", x=w)
                    for hl in range(HB):
                        h = h0 + hl
                        P.op("pe", lambda e, hl=hl, h=h: e.matmul(psb[0][0:64, hl * 64:(hl + 1) * 64],
                                                                 lhsT=kT[:, h, c0:c0 + 64], rhs=kT[:, h, c0:c0 + 64],
                                                                 start=True, stop=True), reads=["kT"], writes=["ps0"])
                    for hl in range(HB):
                        h = h0 + hl
                        P.op("pe", lambda e, hl=hl, h=h: e.matmul(psb[1][0:64, hl * 64:(hl + 1) * 64],
                                                                 lhsT=eoh_sb[32:48, h:h + 1].to_broadcast([16, 64]), rhs=R[32:48, c0:c0 + 64],
                                                                 start=True, stop=True), reads=["eoh", "R"], writes=["ps1"])
                    gci = RT[:, 32 + h0:32 + h0 + HB].unsqueeze(2).to_broadcast([64, HB, 64])
                    bti = RT[:, h0:h0 + HB].unsqueeze(2).to_broadcast([64, HB, 64])
                    B3 = v3(psb[1][0:64, :HB * 64], 64)
                    P.op("dve", lambda e: e.scalar_tensor_tensor(out=Gm[:], in0=B3, scalar=-1.0, in1=gci, op0=ALU.mult,
                                                                 op1=ALU.add), reads=["ps1", "RT"], writes=["Gm"])
                    P.op("dve", lambda e: e.tensor_scalar(out=Gm[:], in0=Gm[:], scalar1=0.0, scalar2=None, op0=ALU.min),
                         reads=["Gm"], writes=["Gm"])
                    P.op("act", lambda e: e.activation(out=Gm[:], in_=Gm[:], func=AF.Exp), reads=["Gm"], writes=["Gm"])
                    if with_o:
                        P.op("dve", lambda e: e.tensor_tensor(out=GT[:], in0=B3, in1=gci, op=ALU.subtract),
                             reads=["ps1", "RT"], writes=["GT"])
                        P.op("dve", lambda e: e.tensor_scalar(out=GT[:], in0=GT[:], scalar1=0.0, scalar2=None,
                                                              op0=ALU.min), reads=["GT"], writes=["GT"])
                        P.op("act", lambda e: e.activation(out=GT[:], in_=GT[:], func=AF.Exp), reads=["GT"],
                             writes=["GT"])
                        P.op("dve", lambda e: e.tensor_tensor(
                            out=GT[:], in0=GT[:], in1=mS_sb[:, 1, :].unsqueeze(1).to_broadcast([64, HB, 64]),
                            op=ALU.mult), reads=["GT", "mS"], writes=["GT"])
                    P.op("dve", lambda e: e.tensor_tensor(
                        out=dmb[:], in0=Gm[:], in1=mS_sb[:, 0, :].unsqueeze(1).to_broadcast([64, HB, 64]), op=ALU.mult),
                        reads=["Gm", "mS"], writes=["dmb"])
                    P.op("dve", lambda e: e.tensor_tensor(out=dmb[:], in0=dmb[:], in1=bti, op=ALU.mult),
                         reads=["dmb", "RT"], writes=["dmb"])
                    A_, AT_ = Abuf[0], Abuf[1]
                    P.op("dve", lambda e: e.tensor_tensor(out=A_[:], in0=v3(psb[0][0:64, :HB * 64], 64), in1=dmb[:],
                                                          op=ALU.mult), reads=["ps0", "dmb"], writes=["Ab0"])
                    if with_o:
                        for hl in range(HB):
                            h = h0 + hl
                            P.op("pe", lambda e, hl=hl, h=h: e.matmul(
                                psb[0][0:64, hl * 64:(hl + 1) * 64], lhsT=kT[:, h, c0:c0 + 64],
                                rhs=qT_t[:, h, c0:c0 + 64], start=True, stop=True), reads=["kT", "qT"], writes=["ps0"])
                        P.op("dve", lambda e: e.tensor_tensor(out=Mq[:], in0=v3(psb[0][0:64, :HB * 64], 64), in1=GT[:],
                                                              op=ALU.mult), reads=["ps0", "GT"], writes=["Mq"])
                    for hl in range(HB):
                        P.op("pe", lambda e, hl=hl: e.transpose(pbf[2][0:64, hl * 64:(hl + 1) * 64], A_[:, hl, :],
                                                                identb[0:64, 0:64]),
                             reads=["Ab0", "identb"], writes=["ps2"])
                    P.op("act", lambda e: e.copy(out=AT_[:], in_=v3(pbf[2][0:64, :HB * 64], 64)), reads=["ps2"],
                         writes=["Ab1"])
                    P.op("dve", lambda e: e.tensor_tensor(
                        out=Tt[:], in0=ident_sb[0:64, 0:64].unsqueeze(1).to_broadcast([64, HB, 64]), in1=AT_[:],
                        op=ALU.subtract), reads=["ident", "Ab1"], writes=["Tt"])
                    P.op("act", lambda e: e.copy(out=Ttb[:], in_=Tt[:]), reads=["Tt"], writes=["Ttb"])
                    pi, qi = 0, 1
                    for lvl in range(5):
                        pn, qn = (2, 3) if pi == 0 else (0, 1)
                        Pc, Qc, Pn, Qn = Abuf[pi], Abuf[qi], Abuf[pn], Abuf[qn]
                        for hl in range(HB):
                            P.op("pe", lambda e, hl=hl, Pc=Pc, Qc=Qc: e.matmul(
                                psb[2][0:64, hl * 64:(hl + 1) * 64], lhsT=Qc[:, hl, :], rhs=Pc[:, hl, :],
                                start=True, stop=True), reads=[f"Ab{pi}", f"Ab{qi}"], writes=["ps2"])
                        for hl in range(HB):
                            P.op("pe", lambda e, hl=hl, Pc=Pc, Qc=Qc: e.matmul(
                                psb[3][0:64, hl * 64:(hl + 1) * 64], lhsT=Pc[:, hl, :], rhs=Qc[:, hl, :],
                                start=True, stop=True), reads=[f"Ab{pi}", f"Ab{qi}"], writes=["ps3"])
                        P.op("act", lambda e, Pn=Pn: e.copy(out=Pn[:], in_=v3(psb[2][0:64, :HB * 64], 64)),
                             reads=["ps2"], writes=[f"Ab{pn}"])
                        P.op("dve", lambda e, Qn=Qn: e.tensor_copy(out=Qn[:], in_=v3(psb[3][0:64, :HB * 64], 64)),
                             reads=["ps3"], writes=[f"Ab{qn}"])
                        for hl in range(HB):
                            P.op("pe", lambda e, hl=hl, Pn=Pn: e.matmul(
                                psb[4][0:64, hl * 64:(hl + 1) * 64], lhsT=Pn[:, hl, :], rhs=Ttb[:, hl, :],
                                start=True, stop=True), reads=[f"Ab{pn}", "Ttb"], writes=["ps4"])
                        P.op("dve", lambda e: e.tensor_tensor(out=Tt[:], in0=Tt[:], in1=v3(psb[4][0:64, :HB * 64], 64),
                                                              op=ALU.add), reads=["Tt", "ps4"], writes=["Tt"])
                        P.op("act", lambda e: e.copy(out=Ttb[:], in_=Tt[:]), reads=["Tt"], writes=["Ttb"])
                        pi, qi = pn, qn
                    for hl in range(HB):
                        h = h0 + hl
                        P.op("pe", lambda e, hl=hl, h=h: e.transpose(pbf[0][0:64, hl * 128:(hl + 1) * 128],
                                                                    kT[:, h, c0:c0 + 64], identb[:]),
                             reads=["kT", "identb"], writes=["ps0"])
                    for hl in range(HB):
                        h = h0 + hl
                        P.op("pe", lambda e, hl=hl, h=h: e.transpose(pbf[1][0:64, hl * 128:(hl + 1) * 128],
                                                                    vT[:, h, c0:c0 + 64], identb[:]),
                             reads=["vT", "identb"], writes=["ps1"])
                    kt3 = v3(pbf[0][0:64, :HB * 128], 128)
                    vt3 = v3(pbf[1][0:64, :HB * 128], 128)
                    bc128 = lambda t: t[:, h0:h0 + HB].unsqueeze(2).to_broadcast([64, HB, 128])
                    P.op("dve", lambda e: e.tensor_tensor(out=Kb[:], in0=kt3, in1=bc128(bege), op=ALU.mult),
                         reads=["ps0", "bege"], writes=["Kb"])
                    P.op("dve", lambda e: e.tensor_tensor(out=kd[:], in0=kt3, in1=bc128(ekd), op=ALU.mult),
                         reads=["ps0", "ekd"], writes=["kd"])
                    P.op("dve", lambda e: e.tensor_tensor(out=Vb[:], in0=vt3, in1=bc128(RT), op=ALU.mult),
                         reads=["ps1", "RT"], writes=["Vb"])
                    for hl in range(HB):
                        P.op("pe", lambda e, hl=hl: e.matmul(pbU[0:64, hl * 128:(hl + 1) * 128], lhsT=Ttb[:, hl, :],
                                                             rhs=Vb[:, hl, :], start=True, stop=True),
                             reads=["Ttb", "Vb"], writes=["ps5", "ps6"])
                    P.op("act", lambda e: e.copy(out=u[:], in_=v3(pbU[0:64, :HB * 128], 128)), reads=["ps5", "ps6"],
                         writes=["u"])
                    for hl in range(HB):
                        P.op("pe", lambda e, hl=hl: e.matmul(psb[4][:, hl * 64:(hl + 1) * 64], lhsT=Kb[:, hl, :],
                                                             rhs=Ttb[:, hl, :], start=True, stop=True),
                             reads=["Kb", "Ttb"], writes=["ps4"])
                    P.op("dve", lambda e: e.tensor_copy(out=WT[:], in_=v3(psb[4][:, :HB * 64], 64)), reads=["ps4"],
                         writes=["WT"])
                    for hl in range(HB):
                        h = h0 + hl
                        P.op("pe", lambda e, hl=hl, h=h: e.matmul(pbU[0:64, hl * 128:(hl + 1) * 128], lhsT=WT[:, hl, :],
                                                                 rhs=Sb[:, h, :], start=True, stop=True),
                             reads=["WT", f"Sb{hb}"], writes=["ps5", "ps6"])
                    P.op("dve", lambda e: e.tensor_tensor(out=vn[:], in0=u[:], in1=v3(pbU[0:64, :HB * 128], 128),
                                                          op=ALU.subtract), reads=["u", "ps5", "ps6"], writes=["vn"])
                    if with_o:
                        for hl in range(HB):
                            h = h0 + hl
                            P.op("pe", lambda e, hl=hl, h=h: e.matmul(
                                psb[7][:, hl * 64:(hl + 1) * 64], lhsT=eoh_sb[32:48, h:h + 1].to_broadcast([16, 128]), rhs=R[32:48, c0:c0 + 64],
                                start=True, stop=True), reads=["eoh", "R"], writes=["ps7"])
                        P.op("act", lambda e: e.activation(out=egb[:], in_=v3(psb[7][:, :HB * 64], 64), func=AF.Exp),
                             reads=["ps7"], writes=["egb"])
                        P.op("dve", lambda e: e.tensor_tensor(out=qg[:], in0=qT_t[:, h0:h0 + HB, c0:c0 + 64], in1=egb[:],
                                                              op=ALU.mult), reads=["qT", "egb"], writes=["qg"])
                        for hl in range(HB):
                            h = h0 + hl
                            P.op("pe", lambda e, hl=hl, h=h: e.matmul(
                                psb[3][:, hl * 64:(hl + 1) * 64], lhsT=Sb[:, h, :], rhs=qg[:, hl, :],
                                start=True, stop=False), reads=[f"Sb{hb}", "qg"], writes=["ps3"])
                            P.op("pe", lambda e, hl=hl: e.matmul(
                                psb[3][:, hl * 64:(hl + 1) * 64], lhsT=vn[:, hl, :], rhs=Mq[:, hl, :],
                                start=False, stop=True), reads=["vn", "Mq"], writes=["ps3"])
                        P.op("act", lambda e: e.copy(out=oT_t[:, h0:h0 + HB, c0:c0 + 64], in_=v3(psb[3][:, :HB * 64], 64)),
                             reads=["ps3"], writes=["oT"])
                    for hl in range(HB):
                        P.op("pe", lambda e, hl=hl: e.matmul(pbU[:, hl * 128:(hl + 1) * 128], lhsT=kd[:, hl, :],
                                                             rhs=vn[:, hl, :], start=True, stop=True),
                             reads=["kd", "vn"], writes=["ps5", "ps6"])
                    Sv = S[:, h0:h0 + HB, :]
                    P.op("dve", lambda e: e.tensor_tensor(
                        out=Sv, in0=Sv, in1=glx[:, h0:h0 + HB].unsqueeze(2).to_broadcast([128, HB, 128]), op=ALU.mult),
                        reads=[f"S{hb}", "glx"], writes=[f"S{hb}"])
                    P.op("dve", lambda e: e.tensor_tensor(out=Sv, in0=Sv, in1=v3(pbU[:, :HB * 128], 128), op=ALU.add),
                         reads=[f"S{hb}", "ps5", "ps6"], writes=[f"S{hb}"])
                    P.op("act", lambda e: e.copy(out=Sb[:, h0:h0 + HB, :], in_=Sv), reads=[f"S{hb}"],
                         writes=[f"Sb{hb}"])

            def norm_tile(src_dram, col0, n):
                rmsnorm_to(lambda kc, t0, nn: hTt[:, kc, t0 - col0:t0 - col0 + nn], lambda t0: "hTt", src_dram, 0,
                           [(col0 + j, min(64, n - j)) for j in range(0, n, 64)], xbuf, sqbuf, rstd)

            return dict(chunk=chunk, bproj_epilogue=bproj_epilogue, norm_tile=norm_tile, hTt=hTt, cm_sb=cm_sb,
                        R=R, eoh_sb=eoh_sb, dn_sb=dn_sb, sqb=sqb, lnb=lnb, kT=kT, vT=vT, pbU=pbU)

        def sample_delta(ph, R, eoh_sb, xs, oT_t, npr):
            qs_f, ks_f, vs_f = xs["qs_f"], xs["ks_f"], xs["vs_f"]
            egs = sb("egs", (128, B_HEADS, NS), F32, ph)
            bts = sb("bts", (128, B_HEADS, NS), F32, ph)
            qkb = sb("qkb", (128, B_HEADS, NS), F32, ph)
            prod = sb("prod", (128, B_HEADS, NS), F32, ph)
            kq = sb("kq", (128, B_HEADS, NS, 2), F32, ph)
            Ss = [sb("Ss0", (128, B_HEADS, 128), F32, ph)]
            ta = sb("ta", (128, B_HEADS), F32, ph)
            vnw = sb("vnw", (128, B_HEADS), F32, ph)
            o1 = sb("o1", (128, B_HEADS), F32, ph)
            dgv = [sb(f"dgv{i}", (128, 128), F32, ph) for i in range(2)]
            dgk = [sb(f"dgk{i}", (128, 128), F32, ph) for i in range(2)]
            vrow = [sb(f"vrow{i}", (128, 128), F32, ph) for i in range(2)]
            for h in range(B_HEADS):
                P.op("pe", lambda e, h=h: e.matmul(psb[0][:, h * NS:(h + 1) * NS], lhsT=eoh_sb[32:48, h:h + 1].to_broadcast([16, 128]),
                                                   rhs=R[32:48, npr:npr + NS], start=True, stop=True),
                     reads=["eoh", "R"], writes=["ps0"])
                P.op("pe", lambda e, h=h: e.matmul(psb[0][:, 256 + h * NS:256 + (h + 1) * NS], lhsT=eoh_sb[0:16, h:h + 1].to_broadcast([16, 128]),
                                                   rhs=R[0:16, npr:npr + NS], start=True, stop=True),
                     reads=["eoh", "R"], writes=["ps0"])
            fl = lambda t: t[:].rearrange("p h s -> p (h s)")
            P.op("act", lambda e: e.activation(out=fl(egs), in_=psb[0][:, 0:256], func=AF.Exp), reads=["ps0"],
                 writes=["egs"])
            P.op("act", lambda e: e.copy(out=fl(bts), in_=psb[0][:, 256:512]), reads=["ps0"], writes=["bts"])
            P.op("dve", lambda e: e.tensor_tensor(out=prod[:], in0=qs_f[:], in1=ks_f[:], op=ALU.mult),
                 reads=["qs_f", "ks_f"], writes=["prod"])
            P.op("pe", lambda e: e.matmul(psb[1][:, 0:256], lhsT=ones_f[:], rhs=fl(prod), start=True, stop=True),
                 reads=["onesf", "prod"], writes=["ps1"])
            P.op("act", lambda e: e.copy(out=fl(qkb), in_=psb[1][:, 0:256]), reads=["ps1"], writes=["qkb"])
            P.op("dve", lambda e: e.tensor_copy(out=kq[:, :, :, 0], in_=ks_f[:]), reads=["ks_f"], writes=["kq"])
            P.op("dve", lambda e: e.tensor_copy(out=kq[:, :, :, 1], in_=qs_f[:]), reads=["qs_f", "kq"], writes=["kq"])
            n2 = 0
            for s_ in range(NS):
                i = 0
                P.dma("sp", Ss[i][:], sdel[s_].rearrange("h k v -> k h v"), writes=[f"Ss{i}"])
                for h in range(B_HEADS):
                    P.op("pe", lambda e, h=h, i=i, s_=s_: e.matmul(psb[2][:, h * 2:h * 2 + 2], lhsT=Ss[i][:, h, :],
                                                                  rhs=kq[:, h, s_, :], start=True, stop=True),
                         reads=[f"Ss{i}", "kq"], writes=["ps2"])
                kqS = psb[2][:, 0:32].rearrange("p (h t) -> p h t", t=2)
                P.op("dve", lambda e, s_=s_: e.tensor_tensor(out=ta[:], in0=kqS[:, :, 0], in1=egs[:, :, s_], op=ALU.mult),
                     reads=["ps2", "egs"], writes=["ta"])
                P.op("dve", lambda e, s_=s_: e.tensor_tensor(out=ta[:], in0=vs_f[:, :, s_], in1=ta[:], op=ALU.subtract),
                     reads=["vs_f", "ta"], writes=["ta"])
                P.op("dve", lambda e, s_=s_: e.tensor_tensor(out=vnw[:], in0=ta[:], in1=bts[:, :, s_], op=ALU.mult),
                     reads=["ta", "bts"], writes=["vnw"])
                P.op("dve", lambda e, s_=s_: e.tensor_tensor(out=o1[:], in0=kqS[:, :, 1], in1=egs[:, :, s_], op=ALU.mult),
                     reads=["ps2", "egs"], writes=["o1"])
                P.op("dve", lambda e, s_=s_: e.tensor_tensor(out=ta[:], in0=vnw[:], in1=qkb[:, :, s_], op=ALU.mult),
                     reads=["vnw", "qkb", "ta"], writes=["ta"])
                P.op("dve", lambda e, s_=s_: e.tensor_tensor(out=oT_t[:, :, npr + s_], in0=o1[:], in1=ta[:], op=ALU.add),
                     reads=["o1", "ta", "oT"], writes=["oT"])
                for h in range(B_HEADS):
                    j = n2 % 2
                    n2 += 1
                    P.op("dve", lambda e, h=h, j=j: e.tensor_scalar(out=dgv[j][:], in0=ident_sb[:], scalar1=vnw[:, h:h + 1],
                                                                    scalar2=None, op0=ALU.mult),
                         reads=["ident", "vnw"], writes=[f"dgv{j}"])
                    P.op("pe", lambda e, j=j: e.matmul(psb[3 + j][:, 0:128], lhsT=ones_f[:], rhs=dgv[j][:], start=True,
                                                       stop=True), reads=["onesf", f"dgv{j}"], writes=[f"ps{3 + j}"])
                    P.op("act", lambda e, j=j: e.copy(out=vrow[j][:], in_=psb[3 + j][:, 0:128]), reads=[f"ps{3 + j}"],
                         writes=[f"vrow{j}"])
                    P.op("dve", lambda e, h=h, j=j, s_=s_: e.tensor_scalar(
                        out=dgk[j][:], in0=ident_sb[:], scalar1=ks_f[:, h, s_:s_ + 1], scalar2=None, op0=ALU.mult),
                        reads=["ident", "ks_f"], writes=[f"dgk{j}"])
                    P.op("pe", lambda e, j=j: e.matmul(psb[5 + j][:, 0:128], lhsT=dgk[j][:], rhs=vrow[j][:], start=True,
                                                       stop=True), reads=[f"dgk{j}", f"vrow{j}"], writes=[f"ps{5 + j}"])
                    P.op("dve", lambda e, h=h, j=j, i=i, s_=s_: e.scalar_tensor_tensor(
                        out=Ss[i][:, h, :], in0=Ss[i][:, h, :], scalar=egs[:, h, s_:s_ + 1], in1=psb[5 + j][:, 0:128],
                        op0=ALU.mult, op1=ALU.add), reads=[f"Ss{i}", "egs", f"ps{5 + j}"], writes=[f"Ss{i}"])
                P.dma("sp", o_Ss[s_].rearrange("h k v -> k h v"), Ss[i][:], reads=[f"Ss{i}"])


        if RUN_B:
            P.dma("pool", o_cst_old, cst_in[:, 1:3, :])
            with contextlib.ExitStack() as ph:
                B1 = bsetup(ph, 512, False)
                hTt = B1["hTt"]
                NCH1 = P1_CHUNKS
                for T in range((NCH1 + 7) // 8):
                    B1["norm_tile"](xallT, T * 512, 512)
                    linear(w_b, list(range(16, 49)), KC, lambda kc, t0, n: hTt[:, kc, t0:t0 + n], lambda t0: "hTt",
                           [(0, 512)], B1["bproj_epilogue"](512, 512, None, False, None))
                    for ch in range(8):
                        j = T * 8 + ch
                        if j >= NCH1:
                            break
                        if DBG_SKIP_CHUNK:
                            continue
                        B1["chunk"](ch * 64, False, None, None)
                        if (j + 2) % 16 == 0:
                            c = (j + 2) // 16
                            for hb in range(B_HEADS // HB):
                                Sv = S[:, hb * HB:(hb + 1) * HB, :]
                                So = S_own[:, hb * HB:(hb + 1) * HB, :]
                                P.op("dve", lambda e, Sv=Sv, So=So, c=c: e.scalar_tensor_tensor(
                                    out=So, in0=Sv, scalar=B1["cm_sb"][:, c:c + 1], in1=So, op0=ALU.mult, op1=ALU.add),
                                    reads=[f"S{hb}", "cm", "S_own"], writes=["S_own"])
                P.flush()

            if not DBG_SKIP_P2:
                with contextlib.ExitStack() as ph:
                    NTK = 256
                    B2 = bsetup(ph, NTK, True)
                    hTt, R, eoh_sb, dn_sb, sqb, lnb = (B2[k] for k in ("hTt", "R", "eoh_sb", "dn_sb", "sqb", "lnb"))
                    qT_t = sb("qT_t", (128, B_HEADS, NTK), BF16, ph)
                    oT_t = sb("oT_t", (128, B_HEADS, NTK), F32, ph)
                    obt = [sb(f"obt{i}", (128, NTK), BF16, ph) for i in range(2)]
                    obf = [sb(f"obf{i}", (128, NTK), F32, ph) for i in range(2)]
                    zs = [sb(f"zs{i}", (128, NTK), F32, ph) for i in range(2)]
                    xs = dict(csth=sb("csth_sb", (128, 48, 3, NS), F32, ph), qs_f=sb("qs_f", (128, B_HEADS, NS), F32, ph),
                              ks_f=sb("ks_f", (128, B_HEADS, NS), F32, ph), vs_f=sb("vs_f", (128, B_HEADS, NS), F32, ph))
                    P.dma("sp", xs["csth"][:], csth, writes=["csth"])
                    P.op("pool", lambda e: e.memset(oT_t[:], 0.0), writes=["oT"])
                    for hb in range(B_HEADS // HB):
                        sl_ = slice(hb * HB, (hb + 1) * HB)
                        P.op("dve", lambda e, sl_=sl_: e.tensor_copy(out=S[:, sl_, :], in_=S_own[:, sl_, :]),
                             reads=["S_own", f"S{hb}"], writes=[f"S{hb}"])
                        P.op("act", lambda e, sl_=sl_: e.copy(out=Sb[:, sl_, :], in_=S_own[:, sl_, :]),
                             reads=["S_own", f"Sb{hb}"], writes=[f"Sb{hb}"])
                    tiles2 = [(0, 256, 256), (256, 256, 256), (512, 256, 256), (768, 256, 256), (1024, NT - 1024, 128)]
                    zc = {"i": 0}
                    for ti2, (col0, n, npr) in enumerate(tiles2):
                        last = (ti2 == len(tiles2) - 1)
                        B2["norm_tile"](xT, col0, n)
                        linear(w_b, list(range(0, 49)), KC, lambda kc, t0, nn: hTt[:, kc, t0:t0 + nn], lambda t0: "hTt",
                               [(0, n)], B2["bproj_epilogue"](n, npr, qT_t, last, xs))
                        for ch in range(npr // 64):
                            if ti2 == 0 and ch == 0:
                                continue
                            B2["chunk"](ch * 64, True, qT_t, oT_t)
                        if last:
                            for hb in range(B_HEADS // HB):
                                P.store(o_S[:, hb * HB:(hb + 1) * HB, :], S[:, hb * HB:(hb + 1) * HB, :],
                                        reads=[f"S{hb}"])
                            P.flush_stores()
                            if not DBG_SKIP_SAMPLE:
                                kT2, vT2 = B2["kT"], B2["vT"]
                                SC = 192
                                P.flush()
                                for buf, nm in ((kT2, "kT"), (vT2, "vT"), (qT_t, "qT")):
                                    P.op("pool", lambda e, buf=buf: e.memset(buf[:, :, SC:SC + 64], 0.0), reads=[nm],
                                         writes=[nm])
                                P.op("pool", lambda e: e.memset(R[0:16, SC:SC + 64], 0.0), reads=["R"], writes=["R"])
                                for s_ in range(DBG_NSAMP):
                                    for buf, nm in ((kT2, "kT"), (vT2, "vT"), (qT_t, "qT")):
                                        P.op("pool", lambda e, buf=buf, s_=s_: e.tensor_copy(
                                            out=buf[:, :, SC:SC + 1], in_=buf[:, :, npr + s_:npr + s_ + 1]),
                                            reads=[nm], writes=[nm])
                                    P.op("dve", lambda e, s_=s_: e.tensor_copy(out=R[0:16, SC:SC + 1],
                                                                              in_=R[0:16, npr + s_:npr + s_ + 1]),
                                         reads=["R"], writes=["R"])
                                    P.op("dve", lambda e, s_=s_: e.tensor_copy(
                                        out=R[32:48, SC:SC + 64], in_=R[32:48, npr + s_:npr + s_ + 1].to_broadcast([16, 64])),
                                        reads=["R"], writes=["R"])
                                    P.dma("sp", S[:], sdel[s_], reads=["S0", "S1"], writes=["S0", "S1"])
                                    for hb in range(B_HEADS // HB):
                                        sl_ = slice(hb * HB, (hb + 1) * HB)
                                        P.op("act", lambda e, sl_=sl_: e.copy(out=Sb[:, sl_, :], in_=S[:, sl_, :]),
                                             reads=[f"S{hb}", f"Sb{hb}"], writes=[f"Sb{hb}"])
                                    if not DBG_SAMPLE_NOCHUNK:
                                        B2["chunk"](SC, True, qT_t, oT_t)
                                    P.op("pool", lambda e, s_=s_: e.tensor_copy(out=oT_t[:, :, npr + s_:npr + s_ + 1],
                                                                               in_=oT_t[:, :, SC:SC + 1]),
                                         reads=["oT"], writes=["oT"])
                                    P.dma("sp", o_Ss[s_], S[:], reads=["S0", "S1"])
                                    P.flush()

                        def ep_z(cb, ti, t0, n_, psap, pres, col0=col0, n=n):
                            h = cb - 49
                            i = zc["i"] % 2
                            zc["i"] += 1
                            P.op("act", lambda e: e.activation(out=zs[i][:, :n], in_=psap, func=AF.Silu), reads=[pres],
                                 writes=[f"zs{i}"])
                            P.op("act", lambda e: e.activation(out=sqb[:, :n], in_=oT_t[:, h, :n], func=AF.Square),
                                 reads=["oT"], writes=["sqb"])
                            P.op("pe", lambda e: e.matmul(psb[7][:, :n], lhsT=ones_bf[:], rhs=sqb[:, :n], start=True,
                                                          stop=True), reads=["ones", "sqb"], writes=["ps7"])
                            P.op("act", lambda e: e.activation(out=lnb[:, :n], in_=psb[7][:, :n], func=AF.Ln, bias=EPS,
                                                               scale=1.0 / 128.0), reads=["ps7"], writes=["lnb"])
                            P.op("act", lambda e: e.activation(out=lnb[:, :n], in_=lnb[:, :n], func=AF.Exp, scale=-0.5),
                                 reads=["lnb"], writes=["lnb"])
                            P.op("dve", lambda e: e.tensor_tensor(out=obf[i][:, :n], in0=oT_t[:, h, :n], in1=lnb[:, :n],
                                                                  op=ALU.mult), reads=["oT", "lnb"], writes=[f"obf{i}"])
                            P.op("dve", lambda e: e.scalar_tensor_tensor(out=obf[i][:, :n], in0=obf[i][:, :n],
                                                                         scalar=dn_sb[:, 0:1], in1=zs[i][:, :n],
                                                                         op0=ALU.mult, op1=ALU.mult),
                                 reads=[f"obf{i}", "dn", f"zs{i}"], writes=[f"obf{i}"])
                            P.op("pool", lambda e: e.tensor_copy(out=obt[i][:, :n], in_=obf[i][:, :n]), reads=[f"obf{i}"],
                                 writes=[f"obt{i}"])
                            P.store(ob_s[h, :, col0:col0 + n], obt[i][:, :n], reads=[f"obt{i}"])
                            P.store(o_ob[h, :, col0:col0 + n], obf[i][:, :n], reads=[f"obf{i}"])

                        linear(w_b, list(range(49, 65)), KC, lambda kc, t0, nn: hTt[:, kc, t0:t0 + nn], lambda t0: "hTt",
                               [(0, n)], ep_z)
                    P.flush()

        P.flush()
        bstack.close()
        if RUN_A:
            with contextlib.ExitStack() as ph:
                xbuf = sb("xbuf", (128, KC, 256), F32, ph)
                sqbuf = sb("sqbuf", (128, KC, 256), BF16, ph)
                rstd = sb("rstd", (128, 256), F32, ph)
                hTh = sb("hTh", (128, KC, 512), BF16, ph)
                cos_h = sb("cos_h", (32, HIST), F32, ph)
                sin_h = sb("sin_h", (32, HIST), F32, ph)
                kst = [sb(f"kst{i}", (128, 512), F32, ph) for i in range(2)]
                kbf = [sb(f"kbf{i}", (128, 512), BF16, ph) for i in range(2)]
                vbf = [sb(f"vbf{i}", (128, 128), BF16, ph) for i in range(2)]
                rtmp = [sb(f"rtmp{i}", (32, 512), F32, ph) for i in range(2)]
                P.dma("sp", cos_h[:], coshT, writes=["cos"])
                P.dma("sp", sin_h[:], sinhT, writes=["sin"])
                for T in range(HIST // 512):
                    rmsnorm_to(lambda kc, t0, n, T=T: hTh[:, kc, t0 - T * 512:t0 - T * 512 + n], lambda t0: "hTh",
                               xhT, 0, [(T * 512, 256), (T * 512 + 256, 256)], xbuf, sqbuf, rstd)
                    groups = (0, 1, 2) if T == 3 else (2,)

                    def ep_hk(cb, ti, t0, n, psap, pres, T=T):
                        g, h = cb // 12, cb % 4
                        i = est["i"] % 2
                        est["i"] += 1
                        P.op("act", lambda e: e.copy(out=kst[i][:, :n], in_=psap), reads=[pres], writes=[f"kst{i}"])
                        rope(kst[i], f"kst{i}", 0, n, cos_h, sin_h, T * 512, rtmp, i)
                        P.op("act", lambda e: e.copy(out=kbf[i][:, :n], in_=kst[i][:, :n]), reads=[f"kst{i}"],
                             writes=[f"kbf{i}"])
                        P.store(kT_s[g * 4 + h, :, T * 512:T * 512 + 512], kbf[i][:], reads=[f"kbf{i}"])

                    def ep_hv(cb, tidx, psap, pres):
                        g, h = cb // 12, cb % 4
                        i = est["i"] % 2
                        est["i"] += 1
                        P.op("act", lambda e: e.copy(out=vbf[i][:], in_=psap), reads=[pres], writes=[f"vbf{i}"])
                        P.store(vtok_s[g * 4 + h, tidx], vbf[i][:], reads=[f"vbf{i}"])

                    linear(w_a, [g * 12 + 4 + h for g in groups for h in range(4)], KC,
                           lambda kc, t0, n: hTh[:, kc, t0:t0 + n], lambda t0: "hTh", [(0, 512)], ep_hk)
                    linear_tm(w_a, [g * 12 + 8 + h for g in groups for h in range(4)], hTh, lambda t0: "hTh",
                              [(j * 128, T * 4 + j) for j in range(4)], ep_hv)
                P.flush()

            o_aT = sb("o_aT", (128, A_HEADS, NT), BF16)
            hstack = contextlib.ExitStack()
            hT = sb("hT", (128, KC, NT), BF16, hstack)
            with contextlib.ExitStack() as ph:
                xbuf = sb("xbuf", (128, KC, 256), F32, ph)
                sqbuf = sb("sqbuf", (128, KC, 256), BF16, ph)
                rstd = sb("rstd", (128, 256), F32, ph)
                rmsnorm_to(lambda kc, t0, n: hT[:, kc, t0:t0 + n], lambda t0: f"hT{t0 // 512}", xT, 0, NORM_TILES,
                           xbuf, sqbuf, rstd)
                P.flush()

            with contextlib.ExitStack() as ph:
                cos_sb = sb("cos_sb", (32, NT), F32, ph)
                sin_sb = sb("sin_sb", (32, NT), F32, ph)
                qkv = [sb(f"qkvst{i}", (128, NT), F32, ph) for i in range(3)]
                qkvb = [sb(f"qkvb{i}", (128, NP), BF16, ph) for i in range(2)]
                vbf = [sb(f"vbf{i}", (128, 128), BF16, ph) for i in range(2)]
                rtmp = [sb(f"rtmp{i}", (32, 512), F32, ph) for i in range(2)]
                P.dma("sp", cos_sb[:], cosT, writes=["cos"])
                P.dma("sp", sin_sb[:], sinT, writes=["sin"])
                cnt = {"b": 0}

                def ep_a(cb, ti, t0, n, psap, pres):
                    g, which, h = cb // 12, (cb % 12) // 4, cb % 4
                    j = cb % 3
                    dst = qkv[j]
                    P.op("act", lambda e: e.copy(out=dst[:, t0:t0 + n], in_=psap), reads=[pres], writes=[f"qkv{j}_{ti}"])
                    if which < 2:
                        r = est["i"] % 2
                        est["i"] += 1
                        rope(dst, f"qkv{j}_{ti}", t0, n, cos_sb, sin_sb, 0, rtmp, r)
                    if ti == len(TOK_TILES) - 1:
                        allr = [f"qkv{j}_{t}" for t in range(len(TOK_TILES))]
                        P.store(o_qkv[cb], dst[:], reads=allr)
                        if which < 2:
                            bi = cnt["b"] % 2
                            cnt["b"] += 1
                            P.op("pool", lambda e: e.tensor_copy(out=qkvb[bi][:], in_=dst[:, :NP]), reads=allr,
                                 writes=[f"qkvb{bi}"])
                            if which == 0:
                                P.store(qT_s[g * 4 + h], qkvb[bi][:], reads=[f"qkvb{bi}"])
                            else:
                                P.store(kT_s[g * 4 + h, :, HIST:HIST + NP], qkvb[bi][:], reads=[f"qkvb{bi}"])

                def ep_v(cb, tidx, psap, pres):
                    g, h = cb // 12, cb % 4
                    i = est["i"] % 2
                    est["i"] += 1
                    P.op("act", lambda e: e.copy(out=vbf[i][:], in_=psap), reads=[pres], writes=[f"vbf{i}"])
                    P.store(vtok_s[g * 4 + h, tidx], vbf[i][:], reads=[f"vbf{i}"])

                linear(w_a, list(range(A_COLS // 128)), KC, lambda kc, t0, n: hT[:, kc, t0:t0 + n],
                       lambda t0: f"hT{t0 // 512}", TOK_TILES, ep_a)
                linear_tm(w_a, [g * 12 + 8 + h for g in range(NG) for h in range(4)], hT, lambda t0: f"hT{t0 // 512}",
                          [(j * 128, HIST // 128 + j) for j in range(NP // 128)], ep_v)
                if RUN_B and RUN_F:
                    gst = [sb(f"gst{i}", (128, NT), BF16, ph) for i in range(2)]

                    def ep_gate(cb, ti, t0, n, psap, pres):
                        i = cb % 2
                        P.op("act", lambda e: e.activation(out=gst[i][:, t0:t0 + n], in_=psap, func=AF.Sigmoid),
                             reads=[pres], writes=[f"gst{i}_{ti}"])
                        if ti == len(TOK_TILES) - 1:
                            P.store(gate_s[cb], gst[i][:], reads=[f"gst{i}_{t}" for t in range(len(TOK_TILES))])

                    linear(w_g, list(range(64)), KC, lambda kc, t0, n: hT[:, kc, t0:t0 + n],
                           lambda t0: f"hT{t0 // 512}", TOK_TILES, ep_gate)
                P.flush()
            hstack.close()

            with contextlib.ExitStack() as ph:
                TR = (10, 13, 25)
                Ksb = [sb(f"Ksb{g}", (128, TR[g] * 128), BF16, ph) for g in range(NG)]
                Vsb = [sb(f"Vsb{g}", (128, TR[g], 128), BF16, ph) for g in range(NG)]
                Qsb = [sb(f"Qsb{g}", (128, NP), BF16, ph) for g in range(NG)]
                mask_f = sb("mask_f", (128, 24, 128), F32, ph)
                mask_b = sb("mask_b", (128, 24, 128), BF16, ph)
                valid_sb = sb("valid_sb", (128, NTILE_A), F32, ph)
                validm = sb("validm", (128, NTILE_A, 128), BF16, ph)
                pT = [sb(f"pT{i}", (128, 4, 128), BF16, ph) for i in range(2)]
                rden = sb("rden", (128, 128), F32, ph)
                P.dma("sp", mask_f[:], maskT, writes=["maskf"])
                P.dma("sp", valid_sb[:], validf, writes=["validf"])
                P.op("pool", lambda e: e.tensor_copy(out=mask_b[:], in_=mask_f[:]), reads=["maskf"], writes=["mask"])
                for i in range(NTILE_A):
                    P.op("dve", lambda e, i=i: e.tensor_scalar(out=validm[:, i, :], in0=ones_bf[:],
                                                               scalar1=valid_sb[:, i:i + 1], scalar2=None, op0=ALU.mult),
                         reads=["ones", "validf"], writes=["validm"])
                sc = {"i": 0, "acc": 0}
                steps = [(g, r) for g in range(NG) for r in range(RMAX[g] + 1)]
                nst = len(steps)
                for h in range(A_HEADS):
                    for g in range(NG):
                        lo = (NTILE_A - TR[g]) * 128
                        P.dma("sp", Ksb[g][:], kT_s[g * 4 + h, :, lo:], writes=[f"K{g}"])
                        P.dma("sp", Vsb[g][:], vtok_s[g * 4 + h, NTILE_A - TR[g]:].rearrange("t k d -> k t d"),
                              writes=[f"V{g}"])
                        P.dma("sp", Qsb[g][:], qT_s[g * 4 + h], writes=[f"Q{g}"])
                    for qt in range(HIST // 128, NTILE_A):
                        qc = (qt - HIST // 128) * 128
                        a = sc["acc"] % 2
                        sc["acc"] += 1
                        numb, denb = psb[4 + a], psb[6 + a]
                        for s0 in range(0, nst, 4):
                            grp = steps[s0:s0 + 4]
                            i = sc["i"] % 2
                            sc["i"] += 1
                            nb = len(grp)
                            for bi, (g, r) in enumerate(grp):
                                kl = (qt - r) - (NTILE_A - TR[g])
                                P.op("pe", lambda e, i=i, bi=bi, g=g, kl=kl, qc=qc: e.matmul(
                                    psb[i][:, bi * 128:(bi + 1) * 128], lhsT=Ksb[g][:, kl * 128:(kl + 1) * 128],
                                    rhs=Qsb[g][:, qc:qc + 128], start=True, stop=True),
                                    reads=[f"K{g}", f"Q{g}"], writes=[f"ps{i}"])
                            P.op("act", lambda e, i=i, nb=nb: e.activation(
                                out=pT[i][:, :nb, :], in_=psb[i][:, :nb * 128].rearrange("p (b q) -> p b q", q=128),
                                func=AF.Exp, scale=HD ** -0.5), reads=[f"ps{i}"], writes=[f"pT{i}"])
                            P.op("dve", lambda e, i=i, nb=nb, m0=s0: e.tensor_tensor(
                                out=pT[i][:, :nb, :], in0=pT[i][:, :nb, :], in1=mask_b[:, m0:m0 + nb, :], op=ALU.mult),
                                reads=[f"pT{i}", "mask"], writes=[f"pT{i}"])
                            for bi, (g, r) in enumerate(grp):
                                kt = qt - r
                                kl = kt - (NTILE_A - TR[g])
                                first = (s0 + bi == 0)
                                last = (s0 + bi == nst - 1)
                                P.op("pe", lambda e, i=i, bi=bi, g=g, kl=kl, first=first, last=last, numb=numb: e.matmul(
                                    numb[:, :128], lhsT=Vsb[g][:, kl, :], rhs=pT[i][:, bi, :], start=first, stop=last),
                                    reads=[f"V{g}", f"pT{i}"], writes=[f"ps{4 + a}"])
                                P.op("pe", lambda e, i=i, bi=bi, kt=kt, first=first, last=last, denb=denb: e.matmul(
                                    denb[:, :128], lhsT=validm[:, kt, :], rhs=pT[i][:, bi, :], start=first, stop=last),
                                    reads=["validm", f"pT{i}"], writes=[f"ps{6 + a}"])
                        P.op("dve", lambda e, denb=denb: e.tensor_scalar(out=rden[:], in0=denb[:, :128], scalar1=1e-30,
                                                                        scalar2=None, op0=ALU.max),
                             reads=[f"ps{6 + a}"], writes=["rden"])
                        P.op("dve", lambda e: e.reciprocal(out=rden[:], in_=rden[:]), reads=["rden"], writes=["rden"])
                        P.op("dve", lambda e, numb=numb, h=h, qc=qc: e.tensor_tensor(
                            out=o_aT[:, h, qc:qc + 128], in0=numb[:, :128], in1=rden[:], op=ALU.mult),
                            reads=[f"ps{4 + a}", "rden"], writes=["o_aT"])
                P.flush()

            with contextlib.ExitStack() as ph:
                qs = sb("qs", (128, 36, NS), F32, ph)
                kvs = [sb(f"kvs{i}", (128, 2, A_HEADS, HD), F32, ph) for i in range(2)]
                KTs = [sb(f"KTs{i}", (128, A_HEADS, 128), F32, ph) for i in range(2)]
                Pm = sb("Pm", (128, 12, NS), F32, ph)
                pnew = sb("pnew", (128, 12, NS), F32, ph)
                t1 = sb("t1", (128, 12, NS), F32, ph)
                t2 = sb("t2", (128, 12, NS), F32, ph)
                nt_ = sb("nt_", (128, A_HEADS, NS), F32, ph)
                dt_ = sb("dt_", (128, A_HEADS, NS), F32, ph)
                P.dma("sp", qs[:], o_qkv[:, :, NP:NT].rearrange("c d s -> d c s"), writes=["qs"])
                SCb, NUMb = psb[0], psb[1]
                scv = SCb[:, :192].rearrange("p (c s) -> p c s", s=NS)
                numv = NUMb[:, :192].rearrange("p (c s) -> p c s", s=NS)
                n = 0
                for s_ in range(NS):
                    for g in range(NG):
                        i = n % 2
                        n += 1
                        dil = DILS[g][1]
                        src = ck[g][s_].rearrange("(m dl) kv h d -> m dl kv h d", dl=dil)[:, 0]
                        P.dma("sp", kvs[i][:], src, writes=[f"kvs{i}"])
                        for h in range(A_HEADS):
                            P.op("pe", lambda e, i=i, h=h: e.transpose(psb[2 + i][:, h * 128:(h + 1) * 128],
                                                                       kvs[i][:, 0, h, :], ident_sb[:]),
                                 reads=[f"kvs{i}", "ident"], writes=[f"ps{2 + i}"])
                        P.op("act", lambda e, i=i: e.copy(out=KTs[i][:], in_=psb[2 + i].rearrange("p (h m) -> p h m", m=128)),
                             reads=[f"ps{2 + i}"], writes=[f"KT{i}"])
                        for h in range(A_HEADS):
                            P.op("pe", lambda e, i=i, h=h, g=g, s_=s_: e.matmul(
                                scv[:, g * 4 + h, s_:s_ + 1], lhsT=KTs[i][:, h, :], rhs=qs[:, g * 12 + h, s_:s_ + 1],
                                start=True, stop=True), reads=[f"KT{i}", "qs"], writes=["ps0"])
                        P.op("act", lambda e, g=g, s_=s_: e.activation(
                            out=Pm[:, g * 4:(g + 1) * 4, s_:s_ + 1], in_=scv[:, g * 4:(g + 1) * 4, s_:s_ + 1],
                            func=AF.Exp, scale=HD ** -0.5), reads=["ps0"], writes=["Pm"])
                        for h in range(A_HEADS):
                            P.op("pe", lambda e, i=i, h=h, g=g, s_=s_: e.matmul(
                                numv[:, g * 4 + h, s_:s_ + 1], lhsT=kvs[i][:, 1, h, :], rhs=Pm[:, g * 4 + h, s_:s_ + 1],
                                start=True, stop=True), reads=[f"kvs{i}", "Pm"], writes=["ps1"])
                qv = qs[:].rearrange("p (g w h) s -> p g w h s", g=NG, w=3)
                t1v = t1[:].rearrange("p (g h) s -> p g h s", g=NG)
                t2v = t2[:].rearrange("p (g h) s -> p g h s", g=NG)
                pnv = pnew[:].rearrange("p (g h) s -> p g h s", g=NG)
                for g in range(NG):
                    P.op("dve", lambda e, g=g: e.tensor_tensor(out=t1v[:, g], in0=qv[:, g, 0], in1=qv[:, g, 1], op=ALU.mult),
                         reads=["qs"], writes=["t1"])
                P.op("pe", lambda e: e.matmul(psb[5][:, :192], lhsT=ones_f[:], rhs=t1[:].rearrange("p c s -> p (c s)"),
                                              start=True, stop=True), reads=["onesf", "t1"], writes=["ps5"])
                P.op("act", lambda e: e.activation(out=pnew[:].rearrange("p c s -> p (c s)"), in_=psb[5][:, :192],
                                                   func=AF.Exp, scale=HD ** -0.5), reads=["ps5"], writes=["pnew"])
                P.op("pe", lambda e: e.matmul(psb[4][:, :192], lhsT=ones_f[:], rhs=Pm[:].rearrange("p c s -> p (c s)"),
                                              start=True, stop=True), reads=["onesf", "Pm"], writes=["ps4"])
                for g in range(NG):
                    P.op("dve", lambda e, g=g: e.tensor_tensor(out=t1v[:, g], in0=pnv[:, g], in1=qv[:, g, 2], op=ALU.mult),
                         reads=["pnew", "qs", "ps5"], writes=["t1"])
                P.op("dve", lambda e: e.tensor_tensor(out=t1[:], in0=numv, in1=t1[:], op=ALU.add),
                     reads=["ps1", "t1"], writes=["t1"])
                P.op("dve", lambda e: e.tensor_tensor(out=t2[:].rearrange("p c s -> p (c s)"), in0=psb[4][:, :192],
                                                      in1=pnew[:].rearrange("p c s -> p (c s)"), op=ALU.add),
                     reads=["ps4", "pnew"], writes=["t2"])
                for (src, dst, nm) in ((t1v, nt_, "nt"), (t2v, dt_, "dt")):
                    P.op("dve", lambda e, src=src, dst=dst: e.tensor_tensor(out=dst[:], in0=src[:, 0], in1=src[:, 1], op=ALU.add),
                         reads=["t1", "t2"], writes=[nm])
                    P.op("dve", lambda e, src=src, dst=dst: e.tensor_tensor(out=dst[:], in0=dst[:], in1=src[:, 2], op=ALU.add),
                         reads=["t1", "t2", nm], writes=[nm])
                P.op("dve", lambda e: e.reciprocal(out=dt_[:], in_=dt_[:]), reads=["dt"], writes=["dt"])
                P.op("dve", lambda e: e.tensor_tensor(out=o_aT[:, :, NP:NT], in0=nt_[:], in1=dt_[:], op=ALU.mult),
                     reads=["nt", "dt"], writes=["o_aT"])
                P.flush()

            with contextlib.ExitStack() as ph:
                oaf = sb("oaf", (128, A_HEADS, NT), F32, ph)
                P.op("act", lambda e: e.copy(out=oaf[:], in_=o_aT[:]), reads=["o_aT"], writes=["oaf"])
                P.dma("sp", o_oa, oaf[:], reads=["oaf"])
                P.flush()
        if RUN_A and RUN_B and RUN_F:
            C0 = NPRE - 2
            NM = NT - C0
            MT = ((C0, 512), (C0 + 512, 512), (C0 + 1024, NM - 1024))
            with contextlib.ExitStack() as ph:
                obT = sb("obT", (128, B_HEADS, NT), BF16, ph)
                mergedT = sb("mergedT", (128, KC, NT), BF16, ph)
                gbuf = [sb(f"gbuf{i}", (128, 512), BF16, ph) for i in range(2)]
                mtmp = [sb(f"mtmp{i}", (128, 512), F32, ph) for i in range(2)]
                xres = [sb(f"xres{i}", (128, 512), F32, ph) for i in range(2)]
                sqm = [sb(f"sqm{i}", (128, 512), BF16, ph) for i in range(2)]
                rs_f = sb("rs_f", (128, NT), F32, ph)
                P.dma("sp", obT[:], ob_s.rearrange("h d t -> d h t"), writes=["obT"])
                mc = {"i": 0}

                def ep_ma(cb, ti, t0, n, psap, pres):
                    i = mc["i"] % 2
                    mc["i"] += 1
                    P.dma("sp", gbuf[i][:, :n], gate_s[cb, :, t0:t0 + n], writes=[f"gbuf{i}"])
                    P.op("dve", lambda e: e.tensor_tensor(out=mergedT[:, cb, t0:t0 + n], in0=psap, in1=gbuf[i][:, :n],
                                                          op=ALU.mult), reads=[pres, f"gbuf{i}"], writes=[f"mg{ti}"])

                def ep_mb(cb, ti, t0, n, psap, pres):
                    i = mc["i"] % 2
                    mc["i"] += 1
                    P.dma("sp", gbuf[i][:, :n], gate_s[32 + cb, :, t0:t0 + n], writes=[f"gbuf{i}"])
                    P.op("dve", lambda e: e.tensor_tensor(out=mtmp[i][:, :n], in0=psap, in1=gbuf[i][:, :n], op=ALU.mult),
                         reads=[pres, f"gbuf{i}"], writes=[f"mtmp{i}"])
                    P.op("dve", lambda e: e.tensor_tensor(out=mergedT[:, cb, t0:t0 + n], in0=mergedT[:, cb, t0:t0 + n],
                                                          in1=mtmp[i][:, :n], op=ALU.add),
                         reads=[f"mtmp{i}", f"mg{ti}"], writes=[f"mg{ti}"])

                linear(w_ao, list(range(32)), 4, lambda kc, t0, n: o_aT[:, kc, t0:t0 + n], lambda t0: "o_aT", MT, ep_ma)
                linear(w_bo, list(range(32)), 16, lambda kc, t0, n: obT[:, kc, t0:t0 + n], lambda t0: "obT", MT, ep_mb)

                def ep_o(cb, ti, t0, n, psap, pres):
                    i = mc["i"] % 2
                    mc["i"] += 1
                    P.dma("sp", xres[i][:, :n], xT[:, cb, t0:t0 + n], writes=[f"xres{i}"])
                    P.op("dve", lambda e: e.tensor_tensor(out=xres[i][:, :n], in0=psap, in1=xres[i][:, :n], op=ALU.add),
                         reads=[pres, f"xres{i}"], writes=[f"xres{i}"])
                    P.op("act", lambda e: e.activation(out=sqm[i][:, :n], in_=xres[i][:, :n], func=AF.Square),
                         reads=[f"xres{i}"], writes=[f"sqm{i}"])
                    P.op("pe", lambda e: e.matmul(psb[7 - ti][:, :n], lhsT=ones_bf[:], rhs=sqm[i][:, :n],
                                                  start=(cb == 0), stop=(cb == KC - 1)),
                         reads=["ones", f"sqm{i}"], writes=[f"ps{7 - ti}"])
                    P.store(xmid_s[cb, :, t0:t0 + n], xres[i][:, :n], reads=[f"xres{i}"])
                    P.flush_stores()

                linear(w_o, list(range(32)), KC, lambda kc, t0, n: mergedT[:, kc, t0:t0 + n],
                       lambda t0: f"mg{(t0 - C0) // 512}", MT, ep_o)
                for ti, (t0, n) in enumerate(MT):
                    P.op("act", lambda e, ti=ti, t0=t0, n=n: e.activation(out=rs_f[:, t0:t0 + n], in_=psb[7 - ti][:, :n],
                                                                         func=AF.Ln, bias=EPS, scale=1.0 / D),
                         reads=[f"ps{7 - ti}"], writes=["rs_f"])
                    P.op("act", lambda e, t0=t0, n=n: e.activation(out=rs_f[:, t0:t0 + n], in_=rs_f[:, t0:t0 + n],
                                                                   func=AF.Exp, scale=-0.5), reads=["rs_f"], writes=["rs_f"])
                P.store(rs_s[:, :], rs_f[:], reads=["rs_f"])
                P.flush()

            with contextlib.ExitStack() as ph:
                h2T = sb("h2T", (128, KC, NM), BF16, ph)
                fcw_sb = sb("fcw_sb", (128, 86, 3), F32, ph)
                fsth_sb = sb("fsth_sb", (128, 86, 2, NS), F32, ph)
                ghalo = sb("ghalo", (128, 86, 2), F32, ph)
                hs2 = contextlib.ExitStack()
                rs_f = sb("rs_f2", (128, NT), F32, hs2)
                xm = [sb(f"xm{i}", (128, NM), F32, hs2) for i in range(2)]
                P.dma("sp", rs_f[:], rs_s[:, :], writes=["rs_f"])
                P.dma("sp", fcw_sb[:], fcw, writes=["fcw"])
                P.dma("sp", fsth_sb[:], fsth, writes=["fsth"])
                P.dma("pool", o_fold, fst_in[:, 1, :])
                for cb in range(KC):
                    i = cb % 2
                    P.dma("sp", xm[i][:], xmid_s[cb, :, C0:NT], writes=[f"xm{i}"])
                    P.op("dve", lambda e, cb=cb, i=i: e.scalar_tensor_tensor(
                        out=h2T[:, cb, :], in0=xm[i][:], scalar=normw_sb[:, KC + cb:KC + cb + 1], in1=rs_f[:, C0:NT],
                        op0=ALU.mult, op1=ALU.mult), reads=[f"xm{i}", "normw", "rs_f"], writes=["h2T"])
                P.flush()
                hs2.close()
                NFT = 3
                FT = [(0, 352), (352, 352), (704, NM - 704)]
                with contextlib.ExitStack() as ph2:
                    aT = sb("aT", (128, 86, 352), BF16, ph2)
                    gpre = [sb(f"gpre{i}", (128, 2 + 352), F32, ph2) for i in range(2)]
                    gacc = [sb(f"gacc{i}", (128, 352), F32, ph2) for i in range(2)]
                    yst = [sb(f"yst{i}", (128, 352), F32, ph2) for i in range(2)]
                    sqy = [sb(f"sqy{i}", (128, 352), BF16, ph2) for i in range(2)]
                    rs_y = sb("rs_y", (128, 352), F32, ph2)
                    fc = {"i": 0}
                    for ft, (f0, fn_) in enumerate(FT):
                        lastft = (ft == NFT - 1)
                        npf = fn_ - NS if lastft else fn_

                        def ep_gu(cb2, ti, t0, n, psap, pres, ft=ft, f0=f0, fn_=fn_, lastft=lastft, npf=npf):
                            cb, isup = cb2 // 2, cb2 % 2
                            i = cb % 2
                            gp, ga = gpre[i], gacc[i]
                            if not isup:
                                P.op("act", lambda e: e.copy(out=gp[:, 2:2 + n], in_=psap), reads=[pres], writes=[f"gpre{i}"])
                                if ft == 0:
                                    P.op("pool", lambda e: e.memset(gp[:, 0:2], 0.0), reads=[f"gpre{i}"], writes=[f"gpre{i}"])
                                else:
                                    P.op("pool", lambda e: e.tensor_copy(out=gp[:, 0:2], in_=ghalo[:, cb, :]),
                                         reads=["ghalo", f"gpre{i}"], writes=[f"gpre{i}"])
                                P.op("pool", lambda e: e.tensor_copy(out=ghalo[:, cb, :], in_=gp[:, npf:npf + 2]),
                                     reads=[f"gpre{i}"], writes=["ghalo"])
                                if lastft:
                                    P.store(o_g[cb], gp[:, npf:npf + 2 + NS], reads=[f"gpre{i}"])
                                P.op("dve", lambda e: e.tensor_scalar(out=ga[:, :n], in0=gp[:, 0:n], scalar1=fcw_sb[:, cb, 0:1],
                                                                      scalar2=None, op0=ALU.mult),
                                     reads=[f"gpre{i}", "fcw"], writes=[f"gacc{i}"])
                                for r in (1, 2):
                                    P.op("dve", lambda e, r=r: e.scalar_tensor_tensor(
                                        out=ga[:, :n], in0=gp[:, r:r + n], scalar=fcw_sb[:, cb, r:r + 1], in1=ga[:, :n],
                                        op0=ALU.mult, op1=ALU.add), reads=[f"gpre{i}", "fcw", f"gacc{i}"], writes=[f"gacc{i}"])
                                if lastft:
                                    P.op("dve", lambda e: e.tensor_scalar(out=ga[:, npf:npf + NS], in0=fsth_sb[:, cb, 0, :],
                                                                          scalar1=fcw_sb[:, cb, 0:1], scalar2=None, op0=ALU.mult),
                                         reads=["fsth", "fcw", f"gacc{i}"], writes=[f"gacc{i}"])
                                    P.op("dve", lambda e: e.scalar_tensor_tensor(
                                        out=ga[:, npf:npf + NS], in0=fsth_sb[:, cb, 1, :], scalar=fcw_sb[:, cb, 1:2],
                                        in1=ga[:, npf:npf + NS], op0=ALU.mult, op1=ALU.add),
                                        reads=["fsth", "fcw", f"gacc{i}"], writes=[f"gacc{i}"])
                                    P.op("dve", lambda e: e.scalar_tensor_tensor(
                                        out=ga[:, npf:npf + NS], in0=gp[:, 2 + npf:2 + npf + NS], scalar=fcw_sb[:, cb, 2:3],
                                        in1=ga[:, npf:npf + NS], op0=ALU.mult, op1=ALU.add),
                                        reads=[f"gpre{i}", "fcw", f"gacc{i}"], writes=[f"gacc{i}"])
                                P.op("act", lambda e: e.activation(out=ga[:, :n], in_=ga[:, :n], func=AF.Silu),
                                     reads=[f"gacc{i}"], writes=[f"gacc{i}"])
                            else:
                                P.op("dve", lambda e: e.tensor_tensor(out=aT[:, cb, :n], in0=psap, in1=ga[:, :n], op=ALU.mult),
                                     reads=[pres, f"gacc{i}"], writes=["aT"])

                        linear(w_gu, list(range(172)), KC, lambda kc, t0, n, f0=f0: h2T[:, kc, f0 + t0:f0 + t0 + n],
                               lambda t0: "h2T", [(0, fn_)], ep_gu)

                        def ep_d(cb, ti, t0, n, psap, pres, f0=f0):
                            i = fc["i"] % 2
                            fc["i"] += 1
                            P.dma("sp", yst[i][:, :n], xmid_s[cb, :, C0 + f0:C0 + f0 + n], writes=[f"yst{i}"])
                            P.op("dve", lambda e: e.tensor_tensor(out=yst[i][:, :n], in0=psap, in1=yst[i][:, :n], op=ALU.add),
                                 reads=[pres, f"yst{i}"], writes=[f"yst{i}"])
                            P.op("act", lambda e: e.activation(out=sqy[i][:, :n], in_=yst[i][:, :n], func=AF.Square),
                                 reads=[f"yst{i}"], writes=[f"sqy{i}"])
                            P.op("pe", lambda e: e.matmul(psb[7][:, :n], lhsT=ones_bf[:], rhs=sqy[i][:, :n],
                                                          start=(cb == 0), stop=(cb == KC - 1)),
                                 reads=["ones", f"sqy{i}"], writes=["ps7"])
                            P.store(y_s[cb, :, f0:f0 + n], yst[i][:, :n], reads=[f"yst{i}"])
                            P.flush_stores()

                        linear(w_d, list(range(32)), 86, lambda kc, t0, n: aT[:, kc, t0:t0 + n], lambda t0: "aT",
                               [(0, fn_)], ep_d)
                        P.op("act", lambda e, fn_=fn_: e.activation(out=rs_y[:, :fn_], in_=psb[7][:, :fn_], func=AF.Ln,
                                                                   bias=EPS, scale=1.0 / D), reads=["ps7"], writes=["rs_y"])
                        P.op("act", lambda e, fn_=fn_: e.activation(out=rs_y[:, :fn_], in_=rs_y[:, :fn_], func=AF.Exp,
                                                                   scale=-0.5), reads=["rs_y"], writes=["rs_y"])
                        for cb in range(KC):
                            i = fc["i"] % 2
                            fc["i"] += 1
                            P.dma("sp", yst[i][:, :fn_], y_s[cb, :, f0:f0 + fn_], writes=[f"yst{i}"])
                            P.op("dve", lambda e, cb=cb, i=i, fn_=fn_: e.scalar_tensor_tensor(
                                out=yst[i][:, :fn_], in0=yst[i][:, :fn_], scalar=normw_sb[:, 2 * KC + cb:2 * KC + cb + 1],
                                in1=rs_y[:, :fn_], op0=ALU.mult, op1=ALU.mult),
                                reads=[f"yst{i}", "normw", "rs_y"], writes=[f"yst{i}"])
                            P.dma("sp", o_y[cb, :, f0:f0 + fn_], yst[i][:, :fn_], reads=[f"yst{i}"])
                    P.flush()
    return nc, declared


def kernel(**inputs):
    f = lambda k: np.asarray(inputs[k], dtype=np.float32)
    x_prompt = f("x_prompt")[0]
    x_sample = f("x_sample")[:, 0]
    w_in = f("w_in")[0]
    nc, declared = build_program()
    normw = np.concatenate([_fm(f("norm_mix")), _fm(f("norm_ffn")), _fm(f("norm_final")[None])], 1)[:, :, 0]
    shared = {"normw": normw, "ident": np.eye(128, dtype=np.float32)}
    xpad = np.concatenate([np.zeros((NPRE + HIST, D), np.float32), x_prompt], 0)
    if RUN_A:
        shared.update({"w_a": _tile_w(w_in, 0, A_COLS), "perm": _perm32(), "maskT": _attn_masks()})
        caches = [f("cache_kv_w128")[0], f("cache_kv_w512")[0], f("cache_kv_w2048")[0]]
    if RUN_B:
        wbg = np.zeros((D, 128), np.float32)
        wbg[:, 0:16] = w_in[:, OFF_BBETA:OFF_BBETA + 16]
        wbg[:, 32:48] = w_in[:, OFF_BALPHA:OFF_BALPHA + 16]
        w_b = np.concatenate([_tile_w(w_in, OFF_BQKV, OFF_BZ), _tile_w(wbg, 0, 128), _tile_w(w_in, OFF_BZ, OFF_BBETA)], 0)
        bg48 = np.zeros((48, 2), np.float32)
        bg48[32:48, 0] = f("dt_bias")[0]
        bg48[32:48, 1] = f("a_log")[0]
        eoh = np.zeros((48, 16), np.float32)
        for h in range(16):
            eoh[h, h] = 1.0
            eoh[32 + h, h] = 1.0
        sel63 = np.zeros((64, 128), np.float32)
        sel63[63, :] = 1.0
        ii = np.arange(64)
        maskS = np.stack([(ii[None, :] < ii[:, None]).astype(np.float32),
                          (ii[:, None] <= ii[None, :]).astype(np.float32)], 1)
        rmask = np.ones((48, 2, 512), np.float32)
        rmask[:, :, ::64] = 0.0
        rmask[:, 1, 128:] = 0.0
        shared.update({"xallT": _fm(x_prompt[:NALL_DECL]), "w_b": w_b,
                       "convw": np.ascontiguousarray(f("conv_qkv")[0].reshape(4, 48, 128).transpose(2, 1, 0)),
                       "bg48": bg48, "dnorm": np.ascontiguousarray(f("delta_norm")[0][:, None]), "eoh": eoh,
                       "sel63": sel63, "maskS": np.ascontiguousarray(maskS), "rmask": rmask})
        cst = f("state_conv_qkv")[0]
        sdl = f("state_delta")[0]
    if RUN_A and RUN_B and RUN_F:
        wgu = np.empty((172, 128, KC, 128), np.float32)
        wgu[0::2] = _tile_w(f("w_gate")[0], 0, DFF)
        wgu[1::2] = _tile_w(f("w_up")[0], 0, DFF)
        shared.update({"w_g": _tile_w(w_in, OFF_GATE, IN_COLS), "w_ao": _tile_w(f("w_a_out")[0], 0, D),
                       "w_bo": _tile_w(f("w_b_out")[0], 0, D), "w_o": _tile_w(f("w_out")[0], 0, D), "w_gu": wgu,
                       "w_d": _tile_w(f("w_down")[0], 0, D),
                       "fcw": np.ascontiguousarray(f("ffn_conv")[0].reshape(3, 86, 128).transpose(2, 1, 0))})
        fst = f("state_ffn_conv")[0]
    in_maps = []
    for c in range(NCORES):
        base = c * OWN
        xc = np.concatenate([xpad[base + HIST:base + HIST + NP], x_sample[c * NS:(c + 1) * NS]], 0)
        m = dict(shared)
        m["xT"] = _fm(xc)
        if RUN_A:
            cosT, sinT = _rope_tables(c)
            coshT, sinhT = _rope_tables_hist(c)
            valid = np.array([1.0 if (8 * c - 17 + i) >= 0 else 0.0 for i in range(NTILE_A)], np.float32)
            m.update({"xhT": _fm(xpad[base:base + HIST]), "cosT": cosT, "sinT": sinT, "coshT": coshT, "sinhT": sinhT,
                      "validf": np.ascontiguousarray(np.broadcast_to(valid[None, :], (128, NTILE_A)))})
            for g in range(NG):
                m[f"cache{g}"] = np.ascontiguousarray(caches[g][c * NS:(c + 1) * NS])
        if RUN_B:
            cm = np.zeros((128, 8), np.float32)
            cm[:, c] = 1.0
            m.update({"cmask": cm,
                      "csth": np.ascontiguousarray(cst[c * NS:(c + 1) * NS].reshape(NS, 3, 48, 128).transpose(3, 2, 1, 0)),
                      "sdel": np.ascontiguousarray(sdl[c * NS:(c + 1) * NS].transpose(0, 2, 1, 3)),
                      "cst_in": np.ascontiguousarray(cst[c * NS:(c + 1) * NS])})
        if RUN_A and RUN_B and RUN_F:
            fs = fst[c * NS:(c + 1) * NS]
            m.update({"fst_in": np.ascontiguousarray(fs),
                      "fsth": np.ascontiguousarray(fs.reshape(NS, 2, 86, 128).transpose(3, 2, 1, 0))})
        in_maps.append({k: m[k] for k in declared})
    res = run_bass_kernel_spmd(nc, in_maps, core_ids=list(range(NCORES)))
    return assemble(res.results)


def assemble(R):
    C0 = NPRE - 2
    def tok_major(a):
        return np.ascontiguousarray(a.transpose(2, 0, 1).reshape(a.shape[2], -1))
    y_p = np.concatenate([tok_major(R[c]["o_y"][:, :, 2:2 + OWN]) for c in range(NCORES)], 0)[None]
    y_s = np.concatenate([tok_major(R[c]["o_y"][:, :, 2 + OWN:2 + OWN + NS]) for c in range(NCORES)], 0)[:, None, :]

    def kv_rows(g, c, cols):
        k = R[c]["o_qkv"][g * 12 + 4:g * 12 + 8][:, :, cols]
        v = R[c]["o_qkv"][g * 12 + 8:g * 12 + 12][:, :, cols]
        return np.stack([k, v], 0).transpose(3, 0, 1, 2)
    kv_p, kv_s = [], []
    for g, (w, _) in enumerate(DILS):
        rows = []
        for c in range(NCORES):
            lo = max(c * OWN, SEQ - w)
            if lo < (c + 1) * OWN:
                rows.append(kv_rows(g, c, slice(NPRE + lo - c * OWN, NPRE + OWN)))
        kv_p.append(np.concatenate(rows, 0)[None, None])
        kv_s.append(np.concatenate([kv_rows(g, c, slice(NP, NT)) for c in range(NCORES)], 0)[None, :, None])
    bpre = [tok_major(R[c]["o_bpre"]) for c in range(NCORES)]
    conv_p = bpre[NCORES - 1][:3][None, None]
    conv_s = np.concatenate([np.concatenate([R[c]["o_cst_old"], bpre[c][3:][:, None, :]], 1) for c in range(NCORES)], 0)[None]
    delta_p = R[NCORES - 1]["o_S"].transpose(1, 0, 2)[None, None]
    delta_s = np.concatenate([R[c]["o_Ss"].transpose(0, 2, 1, 3) for c in range(NCORES)], 0)[None]
    og = [tok_major(R[c]["o_g"]) for c in range(NCORES)]
    ffn_p = og[NCORES - 1][:2][None, None]
    ffn_s = np.concatenate([np.stack([R[c]["o_fold"], og[c][2:]], 1) for c in range(NCORES)], 0)[None]
    outs = (y_p, y_s, kv_p[0], kv_p[1], kv_p[2], conv_p, delta_p, ffn_p,
            kv_s[0], kv_s[1], kv_s[2], conv_s, delta_s, ffn_s)
    return tuple(np.ascontiguousarray(o, dtype=np.float32) for o in outs)
```
